# Optimizing a Trainium2 kernel written in Bass

```python
import math
import jax, jax.numpy as jnp
from jax import lax
import numpy as np

D_MODEL = 1024
BATCH = 4
SEQ = 4096
DEPTH = 1

MIX_WIDTH = D_MODEL
POOL_WIDTH = D_MODEL // 2
POOL_WINDOWS = (2, 4, 8, 16)
POOL_GROUPS = len(POOL_WINDOWS)
POOL_GROUP_DIM = POOL_WIDTH // POOL_GROUPS

N_HEADS = 8
QK_NOPE_DIM = 64
QK_ROPE_DIM = 32
V_HEAD_DIM = 64
QK_HEAD_DIM = QK_NOPE_DIM + QK_ROPE_DIM
ATTN_WIDTH = N_HEADS * V_HEAD_DIM
Q_LORA_RANK = 384
KV_LORA_RANK = 256
ROPE_THETA = 10000.0
Q_BLOCK = 128

IN_WIDTH = POOL_WIDTH + Q_LORA_RANK + KV_LORA_RANK + QK_ROPE_DIM

FFN_HIDDEN = int(math.ceil(8 * D_MODEL / 3 / 256) * 256)

DEEPNORM_ALPHA = (2.0 * DEPTH) ** 0.25
DEEPNORM_BETA = (8.0 * DEPTH) ** -0.25
LN_EPS = 1e-5
RMS_EPS = 1e-6

kernel_name = "hybrid_pool_mla_deepnorm_encoder"


def layer_norm(x, g, b):
    xf = x.astype(jnp.float32)
    mu = jnp.mean(xf, axis=-1, keepdims=True)
    var = jnp.mean(jnp.square(xf - mu), axis=-1, keepdims=True)
    y = (xf - mu) * lax.rsqrt(var + LN_EPS) * g.astype(jnp.float32) + b.astype(jnp.float32)
    return y.astype(x.dtype)


def rms_norm(x, g):
    xf = x.astype(jnp.float32)
    y = xf * lax.rsqrt(jnp.mean(jnp.square(xf), axis=-1, keepdims=True) + RMS_EPS)
    return (y * g.astype(jnp.float32)).astype(x.dtype)


def rope_tables(positions):
    inv_freq = 1.0 / (ROPE_THETA ** (jnp.arange(0, QK_ROPE_DIM, 2, dtype=jnp.float32) / QK_ROPE_DIM))
    ang = positions.astype(jnp.float32)[..., None] * inv_freq
    return jnp.cos(ang)[:, :, None, :], jnp.sin(ang)[:, :, None, :]


def apply_rope(t, cos, sin):
    tf = t.astype(jnp.float32)
    t1, t2 = jnp.split(tf, 2, axis=-1)
    out = jnp.concatenate([t1 * cos - t2 * sin, t2 * cos + t1 * sin], axis=-1)
    return out.astype(t.dtype)


def centred_mean_minus_self(u, window):
    s = u.shape[1]
    uf = u.astype(jnp.float32)
    cs = jnp.concatenate([jnp.zeros_like(uf[:, :1]), jnp.cumsum(uf, axis=1)], axis=1)
    idx = jnp.arange(s)
    lo = jnp.clip(idx - window // 2, 0, s)
    hi = jnp.clip(idx + window - window // 2, 0, s)
    win_sum = jnp.take(cs, hi, axis=1) - jnp.take(cs, lo, axis=1)
    count = (hi - lo).astype(jnp.float32)[None, :, None]
    return (win_sum / count - uf).astype(u.dtype)


def pool_mixer(u, pool_w, pool_scale):
    outs = []
    for g, w in enumerate(POOL_WINDOWS):
        ug = u[..., g * POOL_GROUP_DIM:(g + 1) * POOL_GROUP_DIM]
        pg = centred_mean_minus_self(ug, w)
        outs.append(jnp.einsum('bsc,cd->bsd', pg, pool_w[g]))
    return jnp.concatenate(outs, axis=-1) * pool_scale


def mla_mixer(cq, ckv, kr, cos, sin, q_norm_g, w_q_up, kv_norm_g, w_k_up, w_v_up):
    b, s, _ = cq.shape
    cq = rms_norm(cq, q_norm_g)
    q = jnp.einsum('bsr,re->bse', cq, w_q_up).reshape(b, s, N_HEADS, QK_HEAD_DIM)
    q = jnp.concatenate([q[..., :QK_NOPE_DIM], apply_rope(q[..., QK_NOPE_DIM:], cos, sin)], axis=-1)
    ckv = rms_norm(ckv, kv_norm_g)
    k_nope = jnp.einsum('bsr,re->bse', ckv, w_k_up).reshape(b, s, N_HEADS, QK_NOPE_DIM)
    v = jnp.einsum('bsr,re->bse', ckv, w_v_up).reshape(b, s, N_HEADS, V_HEAD_DIM)
    k_rope = apply_rope(kr[:, :, None, :], cos, sin)
    k = jnp.concatenate([k_nope, jnp.broadcast_to(k_rope, (b, s, N_HEADS, QK_ROPE_DIM))], axis=-1)
    scale = QK_HEAD_DIM ** -0.5
    n_blocks = s // Q_BLOCK
    q_blocks = q.reshape(b, n_blocks, Q_BLOCK, N_HEADS, QK_HEAD_DIM).transpose(1, 0, 2, 3, 4)

    def attend(qb):
        scores = jnp.einsum('bqhd,bkhd->bhqk', qb, k).astype(jnp.float32) * scale
        p = jax.nn.softmax(scores, axis=-1).astype(v.dtype)
        return jnp.einsum('bhqk,bkhd->bqhd', p, v)

    out = lax.map(attend, q_blocks)
    return out.transpose(1, 0, 2, 3, 4).reshape(b, s, ATTN_WIDTH)


def setup_inputs(seed: int = 0) -> dict:
    key = jax.random.key(seed)
    ks = jax.random.split(key, 20)
    L = DEPTH

    def w(k, shape, fan_in, gain=1.0):
        return jax.random.normal(k, shape, jnp.float32) * (fan_in ** -0.5) * gain

    x = jax.random.normal(ks[0], (BATCH, SEQ, D_MODEL), jnp.float32)
    positions = jnp.broadcast_to(jnp.arange(SEQ, dtype=jnp.int32)[None, :], (BATCH, SEQ))
    return {
        "x": x,
        "positions": positions,
        "w_in": w(ks[1], (L, D_MODEL, IN_WIDTH), D_MODEL),
        "pool_w": w(ks[2], (L, POOL_GROUPS, POOL_GROUP_DIM, POOL_GROUP_DIM), POOL_GROUP_DIM, DEEPNORM_BETA),
        "pool_scale": 1.0 + 0.01 * jax.random.normal(ks[3], (L, POOL_WIDTH), jnp.float32),
        "q_norm_g": 1.0 + 0.01 * jax.random.normal(ks[4], (L, Q_LORA_RANK), jnp.float32),
        "w_q_up": w(ks[5], (L, Q_LORA_RANK, N_HEADS * QK_HEAD_DIM), Q_LORA_RANK),
        "kv_norm_g": 1.0 + 0.01 * jax.random.normal(ks[6], (L, KV_LORA_RANK), jnp.float32),
        "w_k_up": w(ks[7], (L, KV_LORA_RANK, N_HEADS * QK_NOPE_DIM), KV_LORA_RANK),
        "w_v_up": w(ks[8], (L, KV_LORA_RANK, N_HEADS * V_HEAD_DIM), KV_LORA_RANK, DEEPNORM_BETA),
        "w_o": w(ks[9], (L, MIX_WIDTH, D_MODEL), MIX_WIDTH, DEEPNORM_BETA),
        "ln1_g": 1.0 + 0.01 * jax.random.normal(ks[10], (L, D_MODEL), jnp.float32),
        "ln1_b": 0.01 * jax.random.normal(ks[11], (L, D_MODEL), jnp.float32),
        "w_gate": w(ks[12], (L, D_MODEL, FFN_HIDDEN), D_MODEL, DEEPNORM_BETA),
        "w_up": w(ks[13], (L, D_MODEL, FFN_HIDDEN), D_MODEL, DEEPNORM_BETA),
        "w_down": w(ks[14], (L, FFN_HIDDEN, D_MODEL), FFN_HIDDEN, DEEPNORM_BETA),
        "ln2_g": 1.0 + 0.01 * jax.random.normal(ks[15], (L, D_MODEL), jnp.float32),
        "ln2_b": 0.01 * jax.random.normal(ks[16], (L, D_MODEL), jnp.float32),
    }


def reference(x, positions, w_in, pool_w, pool_scale, q_norm_g, w_q_up, kv_norm_g, w_k_up, w_v_up,
              w_o, ln1_g, ln1_b, w_gate, w_up, w_down, ln2_g, ln2_b):
    cos, sin = rope_tables(positions)
    o_q = POOL_WIDTH
    o_kv = o_q + Q_LORA_RANK
    o_kr = o_kv + KV_LORA_RANK
    for l in range(DEPTH):
        h = jnp.einsum('bsd,de->bse', x, w_in[l])
        pool_out = pool_mixer(h[..., :o_q], pool_w[l], pool_scale[l])
        attn_out = mla_mixer(h[..., o_q:o_kv], h[..., o_kv:o_kr], h[..., o_kr:], cos, sin,
                             q_norm_g[l], w_q_up[l], kv_norm_g[l], w_k_up[l], w_v_up[l])
        mix = jnp.einsum('bse,ed->bsd', jnp.concatenate([pool_out, attn_out], axis=-1), w_o[l])
        x = layer_norm(DEEPNORM_ALPHA * x + mix, ln1_g[l], ln1_b[l])
        gate = jnp.einsum('bsd,df->bsf', x, w_gate[l])
        up = jnp.einsum('bsd,df->bsf', x, w_up[l])
        ffn = jnp.einsum('bsf,fd->bsd', jax.nn.silu(gate) * up, w_down[l])
        x = layer_norm(DEEPNORM_ALPHA * x + ffn, ln2_g[l], ln2_b[l])
    return x
```

```python
import math
import numpy as np
import concourse.bass as bass
import concourse.mybir as mybir
from concourse.bass_utils import run_bass_kernel_spmd

F32 = mybir.dt.float32
BF16 = mybir.dt.bfloat16
I32 = mybir.dt.int32
ALU = mybir.AluOpType
AF = mybir.ActivationFunctionType

D = 1024
S = 4096
T = 2048
H = 8
FF = 2816
NFC = 22
WIN_COLS = 1216
ALPHA = 2.0 ** 0.25
LN_EPS = 1e-5
RMS_EPS = 1e-6
SM_SCALE = 96.0 ** -0.5
TWO_PI = 2.0 * math.pi
PI_LO = 3.1415925
CW1 = 6.28125
CW2 = TWO_PI - CW1
NVEC = 28
V_PSC, V_QG, V_KVG, V_G1, V_B1, V_RA, V_RB, V_RA2 = 0, 4, 7, 9, 17, 25, 26, 27

ENGS = ("pe", "act", "dve", "pool", "sp")
SAME_ENGINE_ALL = True


class Op:
    __slots__ = ("eng", "seq", "fn", "waits", "needs_inc", "inc_val", "clock", "is_dma")

    def __init__(self, eng, seq, fn, is_dma=False):
        self.eng = eng
        self.seq = seq
        self.fn = fn
        self.waits = []
        self.needs_inc = False
        self.inc_val = None
        self.clock = None
        self.is_dma = is_dma


class Prog:
    def __init__(self, nc, n_vq=8):
        self.nc = nc
        self.n_vq = n_vq
        self.vqs = ["vq%d" % i for i in range(n_vq)]
        self.streams = list(ENGS) + self.vqs
        self.issue = {e: [] for e in ENGS}
        self.count = {s: 0 for s in self.streams}
        self.clock = {e: {s: 0 for s in self.streams} for e in ENGS}
        self.res_w = {}
        self.res_r = {}
        self.vq_next = 0
        self.vq_last = {q: None for q in self.vqs}
        self.sw_streams = []

    def _deps(self, reads, writes):
        deps = []
        for r in reads:
            w = self.res_w.get(r)
            if w is not None:
                deps.append((w, "raw"))
        for w_ in writes:
            w = self.res_w.get(w_)
            if w is not None:
                deps.append((w, "waw"))
            for rd in self.res_r.get(w_, ()):
                deps.append((rd, "war"))
        return deps

    def _add_waits(self, op, issue_eng, deps):
        clk = self.clock[issue_eng]
        best = {}
        for d, kind in deps:
            if d.eng == issue_eng and not d.is_dma:
                if issue_eng == "pe" or (kind != "raw" and not SAME_ENGINE_ALL):
                    continue
            if clk.get(d.eng, 0) >= d.seq:
                continue
            if d.eng not in best or best[d.eng].seq < d.seq:
                best[d.eng] = d
        for d in best.values():
            if clk.get(d.eng, 0) >= d.seq:
                continue
            op.waits.append(d)
            d.needs_inc = True
            clk[d.eng] = max(clk.get(d.eng, 0), d.seq)
            for s, v in d.clock.items():
                if clk.get(s, 0) < v:
                    clk[s] = v

    def _commit(self, op, reads, writes):
        for r in reads:
            self.res_r.setdefault(r, []).append(op)
        for w in writes:
            self.res_w[w] = op
            self.res_r[w] = []

    def op(self, eng, fn, reads=(), writes=()):
        self.count[eng] += 1
        o = Op(eng, self.count[eng], fn)
        self._add_waits(o, eng, self._deps(reads, writes))
        o.clock = dict(self.clock[eng])
        self.issue[eng].append(o)
        self._commit(o, reads, writes)
        return o

    def dma(self, fn, reads=(), writes=(), eng="sp"):
        if eng == "pool":
            q = "sw%d" % len(self.sw_streams)
            self.sw_streams.append(q)
            self.streams.append(q)
            self.count[q] = 0
            self.vq_last[q] = None
            for e in ENGS:
                self.clock[e][q] = 0
        else:
            q = self.vqs[self.vq_next]
            self.vq_next = (self.vq_next + 1) % self.n_vq
        self.count[q] += 1
        o = Op(q, self.count[q], fn, is_dma=True)
        deps = self._deps(reads, writes)
        prev = self.vq_last[q]
        if prev is not None:
            deps.append((prev, "raw"))
        self._add_waits(o, eng, deps)
        o.clock = dict(self.clock[eng])
        self.vq_last[q] = o
        self.issue[eng].append(o)
        self._commit(o, reads, writes)
        return o

    def barrier(self):
        last = {}
        for e in ENGS:
            for o in self.issue[e]:
                if o.fn is None:
                    continue
                if o.eng not in last or last[o.eng].seq < o.seq:
                    last[o.eng] = o
        for e in ENGS:
            self.count[e] += 1
            o = Op(e, self.count[e], None)
            deps = [(d, "raw") for d in last.values()]
            self._add_waits(o, e, deps)
            o.clock = dict(self.clock[e])
            self.issue[e].append(o)
        self.res_w = {}
        self.res_r = {}

    def emit(self):
        nc = self.nc
        per_stream = {s: [] for s in self.streams}
        for e in ENGS:
            for o in self.issue[e]:
                per_stream[o.eng].append(o)
        sems = {s: nc.alloc_semaphore("sem_" + s) for s in self.streams
                if any(o.needs_inc for o in per_stream[s])}
        for s, lst in per_stream.items():
            lst.sort(key=lambda o: o.seq)
            c = 0
            for o in lst:
                if o.needs_inc:
                    c += 16 if o.is_dma else 1
                    o.inc_val = c
        prog = self

        def run(engname):
            def body(eng):
                for o in prog.issue[engname]:
                    for d in o.waits:
                        eng.wait_ge(sems[d.eng], d.inc_val)
                    if o.fn is None:
                        if o.needs_inc:
                            eng.nop().then_inc(sems[o.eng], 1)
                        continue
                    ins = o.fn(eng)
                    if o.needs_inc:
                        ins.then_inc(sems[o.eng], 16 if o.is_dma else 1)
            return body

        with nc.Block() as block:
            block.tensor(run("pe"))
            block.scalar(run("act"))
            block.vector(run("dve"))
            block.gpsimd(run("pool"))
            block.sync(run("sp"))


class Arena:
    def __init__(self, nc, base, top):
        self.nc = nc
        self.base = base
        self.top = top
        self.ptr = base
        self.n = 0

    def mark(self):
        return self.ptr

    def release(self, m):
        self.ptr = m

    def tile(self, shape, dtype, name=None):
        nbytes = int(np.prod(shape[1:])) * mybir.dt.size(dtype)
        nbytes = (nbytes + 31) // 32 * 32
        off = self.ptr
        assert off + nbytes <= self.top, ("SBUF overflow", name, off + nbytes - self.top)
        self.ptr += nbytes
        self.n += 1
        return self.nc.alloc_sbuf_tensor_at("%s_%d" % (name or "t", self.n), list(shape), dtype, offset=off)


def build_nc(debug=False):
    nc = bass.Bass("TRN2", target_bir_lowering=False)
    taps = {}

    def tap(name, tile_ap, shape, dt, reads=()):
        if not debug:
            return
        o = nc.dram_tensor("dbg_" + name, list(shape), dt, kind="ExternalOutput").ap()
        P.dma(lambda e: e.dma_start(out=o, in_=tile_ap), list(reads), ["dbg_" + name])
        taps[name] = 1

    def din(name, shape, dt=F32):
        return nc.dram_tensor(name, list(shape), dt, kind="ExternalInput").ap()

    xT_l = din("xT_l", [8, 128, 2, 2048])
    xown = din("xown", [T, D])
    xh_l = din("xh_l", [128, 128])
    pos = nc.dram_tensor("pos", [1, S], I32, kind="ExternalInput")
    ratio = nc.dram_tensor("ratio", [1, 64], F32, kind="ExternalInput")
    vecs_d = din("vecs", [128, NVEC])
    winq_d = din("winq_l", [128, 2, 1536])
    winkv_d = din("winkv_l", [128, 2, 1280])
    winp_d = din("winp_l", [128, 2, 2048])
    poolw_d = din("pool_w_l", [128, 4, 128])
    wq_d = din("wq_l", [128, 2, 1536])
    wk_d = din("wk_l", [128, 1, 2048])
    wv_d = din("wv_l", [128, 2, 512])
    wo_d = din("wo_l", [128, 4, 2048])
    wgu_d = din("wgu_l", [NFC, 128, 2, 8, 128])
    wd_d = din("wd_l", [NFC, 128, 1024])
    lnp = nc.dram_tensor("lnp", [4, D], F32, kind="ExternalInput")
    y = nc.dram_tensor("y", [T, D], F32, kind="ExternalOutput").ap()

    P = Prog(nc)
    base = (nc.sbuf_base + 63) // 64 * 64
    A = Arena(nc, base, nc.sbuf_top)
    ps = [nc.alloc_psum_tensor("psb%d" % i, [128, 512], F32) for i in range(8)]
    PSN = ["ps%d" % i for i in range(8)]

    def mm(out, lhsT, rhs, start, stop, reads, writes):
        P.op("pe", lambda e: e.matmul(out, lhsT=lhsT, rhs=rhs, start=start, stop=stop), reads, writes)

    def act(out, in_, func, reads, writes, scale=1.0, bias=0.0):
        P.op("act", lambda e: e.activation(out=out, in_=in_, func=func, scale=scale, bias=bias), reads, writes)

    def tt(eng, out, in0, in1, op, reads, writes):
        P.op(eng, lambda e: e.tensor_tensor(out=out, in0=in0, in1=in1, op=op), reads, writes)

    def ts(eng, out, in0, s1, s2, op0, op1, reads, writes):
        if s2 is None:
            P.op(eng, lambda e: e.tensor_scalar(out=out, in0=in0, scalar1=s1, scalar2=None, op0=op0), reads, writes)
        else:
            P.op(eng, lambda e: e.tensor_scalar(out=out, in0=in0, scalar1=s1, scalar2=s2, op0=op0, op1=op1), reads, writes)

    def stt(out, in0, scalar, in1, op0, op1, reads, writes):
        P.op("dve", lambda e: e.scalar_tensor_tensor(out=out, in0=in0, scalar=scalar, in1=in1, op0=op0, op1=op1), reads, writes)

    def cp(eng, out, in_, reads, writes):
        P.op(eng, lambda e: e.tensor_copy(out=out, in_=in_), reads, writes)

    def memset(eng, ap, val, writes):
        P.op(eng, lambda e: e.memset(ap, val), (), writes)

    def dma(out, in_, reads, writes, eng="sp"):
        P.dma(lambda e: e.dma_start(out=out, in_=in_), reads, writes, eng=eng)

    vecs = A.tile([128, NVEC], F32, "vecs")
    ident = A.tile([128, 128], F32, "ident")
    ones = A.tile([128, 128], BF16, "ones")
    neghalf = A.tile([128, 1], F32, "neghalf")
    mixT = A.tile([128, 8, T], BF16, "mixT")
    m_whole = A.mark()

    dma(vecs[:, :], vecs_d[:, :], ["vecs_d"], ["vecs"])

    hpool = A.tile([128, 4, 2064], F32, "hpool")
    cqn = A.tile([128, 3, T], BF16, "cqn")
    ckvn = A.tile([128, 2, S], BF16, "ckvn")
    krot = A.tile([128, S], BF16, "krot")
    CS = A.tile([128, S], F32, "CS")
    wqb = A.tile([128, 3, 1024], BF16, "wqb")
    wkb = A.tile([128, 2, 1024], BF16, "wkb")
    wvb = A.tile([128, 2, 512], BF16, "wvb")
    poolwb = A.tile([128, 4, 128], BF16, "poolwb")
    ratio_bc = A.tile([128, 64], F32, "ratio")
    m_ab = A.mark()

    NXB = 3
    xTb = [A.tile([128, 8, 512], BF16, "xTb") for _ in range(NXB)]
    O_WQ = A.mark() - m_ab
    winb_q = A.tile([128, 8, 384], BF16, "winb_q")
    winb_kv = A.tile([128, 8, 320], BF16, "winb_kv")
    O_WP = A.mark() - m_ab
    winb_p = A.tile([128, 8, 512], BF16, "winb_p")
    xhb = A.tile([128, 8, 16], BF16, "xhb")
    sq = [A.tile([128, 512], BF16, "sq") for _ in range(5)]
    rstd = [A.tile([128, 512], F32, "rstd") for _ in range(2)]
    lnt = [A.tile([128, 512], F32, "lnt") for _ in range(2)]
    rtA = A.tile([128, 512], F32, "rtA")
    rtB = A.tile([128, 512], F32, "rtB")
    O_RT = A.mark() - m_ab
    posi = A.tile([128, 1024], I32, "posi")
    angf = A.tile([128, 1024], F32, "angf")
    kf = A.tile([128, 1024], F32, "kf")
    ki = A.tile([128, 1024], I32, "ki")
    assert (O_WQ, O_WP, O_RT) == (24576, 35840, 61696) and A.mark() - m_ab == 78080

    def big(ap3, n):
        return ap3.rearrange("p (n a) b -> p n (a b)", n=n)

    def load_x(tg):
        b = tg % NXB
        dma(big(xTb[b][:, :, :], 2), xT_l[tg], ["xT"], ["xTb%d" % b], eng="pool")

    vx0, vq0 = big(xTb[0][:, :, :], 2), big(winb_q[:, :, :], 2)
    dma(vx0[:, 0:1, :], xT_l[0][:, 0:1, :], ["xT"], ["xTb0"], eng="pool")
    dma(vq0[:, 0:1, :], winq_d[:, 0:1, :], ["win_d"], ["winb_q"], eng="pool")
    dma(vx0[:, 1:2, :], xT_l[0][:, 1:2, :], ["xT"], ["xTb0h"], eng="pool")
    dma(vq0[:, 1:2, :], winq_d[:, 1:2, :], ["win_d"], ["winb_qh"], eng="pool")
    dma(big(winb_kv[:, :, :], 2), winkv_d[:, :, :], ["win_d"], ["winb_kv"], eng="pool")
    dma(big(winb_p[:, :, :], 2), winp_d[:, :, :], ["win_d"], ["winb_p"], eng="pool")
    dma(xhb[:, :, :].rearrange("p k s -> p (k s)"), xh_l[:, :], ["xhT"], ["xhb"], eng="pool")
    load_x(1)
    memset("pool", ones[:, :], 1.0, ["ones"])
    memset("pool", neghalf[:, :], -0.5, ["neghalf"])
    memset("pool", ident[:, :], 1.0, ["ident"])
    P.op("pool", lambda e: e.affine_select(out=ident[:, :], in_=ident[:, :], pattern=[[-1, 128]],
                                           compare_op=ALU.is_equal, fill=0.0, base=0, channel_multiplier=1),
         ["ident"], ["ident"])
    dma(wqb[:, :, :].rearrange("p k (n c) -> p (k n) c", n=2).rearrange("p (n a) c -> p n (a c)", n=2), wq_d[:, :, :], ["wq_d"], ["wqb"], eng="pool")
    dma(wkb[:, :, :].rearrange("p k c -> p (k c)").rearrange("p (o c) -> p o c", o=1), wk_d[:, :, :], ["wk_d"], ["wkb"], eng="pool")
    dma(wvb[:, :, :], wv_d[:, :, :], ["wv_d"], ["wvb"], eng="pool")
    dma(poolwb[:, :, :], poolw_d[:, :, :], ["poolw_d"], ["poolwb"], eng="pool")
    dma(ratio_bc[:, :], bass.AP(ratio, 0, [[0, 128], [1, 64]]), ["ratio_d"], ["ratio"])

    def rope_round(r):
        ops = []
        c_lo, c_hi = 2 * r, 2 * r + 1
        rn = "angf"

        def o(f):
            ops.append(f)
        o(lambda: dma(posi[0:64, :], bass.AP(pos, c_lo * 1024, [[0, 64], [1, 1024]]), ["pos_d"], ["posi_a"]))
        o(lambda: dma(posi[64:128, :], bass.AP(pos, c_hi * 1024, [[0, 64], [1, 1024]]), ["pos_d"], ["posi_b"]))
        o(lambda: cp("dve", angf[:, :], posi[:, :], ["posi_a", "posi_b"], [rn]))
        o(lambda: ts("dve", angf[:, :], angf[:, :], vecs[:, V_RA:V_RA + 1], None, ALU.mult, None, [rn, "vecs"], [rn]))
        o(lambda: ts("dve", ki[:, :], angf[:, :], 1.0 / TWO_PI, None, ALU.mult, None, [rn], ["ki"]))
        o(lambda: cp("dve", kf[:, :], ki[:, :], ["ki"], ["kf"]))
        o(lambda: stt(angf[:, :], kf[:, :], -CW1, angf[:, :], ALU.mult, ALU.add, ["kf", rn], [rn]))
        o(lambda: stt(angf[:, :], kf[:, :], -CW2, angf[:, :], ALU.mult, ALU.add, ["kf", rn], [rn]))
        o(lambda: cp("dve", kf[:, :], posi[:, :], ["posi_a", "posi_b", "kf"], ["kf"]))
        o(lambda: stt(angf[:, :], kf[:, :], vecs[:, V_RA2:V_RA2 + 1], angf[:, :], ALU.mult, ALU.add, ["kf", rn, "vecs"], [rn]))
        o(lambda: ts("dve", angf[:, :], angf[:, :], vecs[:, V_RB:V_RB + 1], None, ALU.add, None, [rn, "vecs"], [rn]))
        o(lambda: ts("dve", kf[:, :], angf[:, :], math.pi, TWO_PI, ALU.is_gt, ALU.mult, [rn], ["kf"]))
        o(lambda: tt("dve", angf[:, :], angf[:, :], kf[:, :], ALU.subtract, [rn, "kf"], [rn]))
        o(lambda: ts("dve", angf[:, :], angf[:, :], -PI_LO, PI_LO, ALU.max, ALU.min, [rn], [rn]))
        o(lambda: act(CS[0:64, c_lo * 1024:(c_lo + 1) * 1024], angf[0:64, :], AF.Sin, [rn], ["CS%d" % c_lo]))
        o(lambda: act(CS[0:64, c_hi * 1024:(c_hi + 1) * 1024], angf[64:128, :], AF.Sin, [rn], ["CS%d" % c_hi]))
        return ops

    for f_ in rope_round(0):
        f_()
    rope_later = rope_round(1)

    def rope_tick():
        if rope_later:
            rope_later.pop(0)()

    bank_rr = [0]

    def next_bank():
        b = bank_rr[0]
        bank_rr[0] = (b + 1) % 8
        return b

    proj_n = [0]

    def proj(bank, wt, col0, M, xb, xbn, wres, ncols=512, kcs=8):
        for kc in range(kcs):
            extra = ([xbn + "h"] + ([wres + "h"] if wres == "winb_q" else [])) if kc >= 4 else []
            mm(ps[bank][0:M, 0:ncols], wt[:, kc, col0:col0 + M], xb[:, kc, :], kc == 0, kc == kcs - 1,
               [wres, xbn] + extra, [PSN[bank]])
        proj_n[0] += 1
        if proj_n[0] > 11:
            rope_tick()

    def norm_chain(banks, sqs, nck, inv_n, gcol, dst, tcols, parity, dstname):
        bs = next_bank()
        for j in range(nck):
            mm(ps[bs][:, :], ones[:, :], sqs[j][:, :], j == 0, j == nck - 1, ["ones", "sq%d" % id(sqs[j])], [PSN[bs]])
        act(lnt[parity][:, :], ps[bs][:, :], AF.Ln, [PSN[bs]], ["lnt%d" % parity], scale=inv_n, bias=RMS_EPS)
        act(rstd[parity][:, :], lnt[parity][:, :], AF.Exp, ["lnt%d" % parity], ["rstd%d" % parity], scale=-0.5)
        for j in range(nck):
            stt(dst[:, j, tcols], ps[banks[j]][:, :], vecs[:, gcol + j:gcol + j + 1], rstd[parity][:, :],
                ALU.mult, ALU.mult, [PSN[banks[j]], "rstd%d" % parity, "vecs"], [dstname])

    def at(off, shape, dt, name):
        return nc.alloc_sbuf_tensor_at(name, list(shape), dt, offset=m_ab + off)

    khT = [at(O_RT, [128, S], BF16, "khT0"), at(0, [128, S], BF16, "khT1")]
    qhT = [at(O_RT + 8192, [128, T], BF16, "qhT0"), at(16384, [128, T], BF16, "qhT1")]
    Vh = [at(O_WP, [128, 32, 128], BF16, "Vh0"), at(8192, [128, 32, 128], BF16, "Vh1")]
    qtA = [at(O_RT + 12288, [128, 512], F32, "qtA0"), at(O_WQ, [128, 512], F32, "qtA1")]
    qtB = [at(O_RT + 14336, [128, 512], F32, "qtB0"), at(O_WQ + 2048, [128, 512], F32, "qtB1")]
    rc = [at(20480, [128, 512], F32, "rc0"), at(22528, [128, 512], F32, "rc1")]
    NPT = 5
    pT = [at(28672 + 1024 * i_, [128, 512], BF16, "pT%d" % i_) for i_ in range(NPT)]
    tmpE = at(44032, [128, 2064], F32, "tmpE")
    pg = [at(52288, [128, T], BF16, "pg0"), at(56384, [128, T], BF16, "pg1")]

    pj_rr = [0]
    pj_nb = [8]

    def pj_bank():
        if pj_nb[0] == 8:
            return next_bank()
        b = 6 + pj_rr[0] % 2
        pj_rr[0] += 1
        return b

    def q_piece(h, tg):
        hb = h % 2
        qhn = "qhT%d" % hb
        b = pj_bank()
        tcols = slice(tg * 512, (tg + 1) * 512)
        for kc in range(3):
            mm(ps[b][:, :], wqb[:, kc, h * 128:(h + 1) * 128], cqn[:, kc, tcols], kc == 0, kc == 2,
               ["wqb", "cqn%d" % tg], [PSN[b]])
        s_ = tg % 2
        cp("dve", qhT[hb][:, tcols], ps[b][:, :], [PSN[b]], [qhn])
        tt("dve", qtA[s_][0:32, :], ps[b][0:32, :], CS[0:32, tcols], ALU.mult, [PSN[b], "CS%d" % (tg // 2)], ["qtA%d" % s_])
        tt("dve", qtB[s_][0:32, :], ps[b][32:64, :], CS[32:64, tcols], ALU.mult, [PSN[b], "CS%d" % (tg // 2)], ["qtB%d" % s_])
        tt("dve", qhT[hb][0:32, tcols], qtA[s_][0:32, :], qtB[s_][0:32, :], ALU.add, ["qtA%d" % s_, "qtB%d" % s_, qhn], [qhn])

    def k_piece(h, kg):
        hb = h % 2
        khn = "khT%d" % hb
        b = pj_bank()
        kcols = slice(kg * 512, (kg + 1) * 512)
        for kc in range(2):
            mm(ps[b][:, :], wkb[:, kc, h * 128:(h + 1) * 128], ckvn[:, kc, kcols], kc == 0, kc == 1,
               ["wkb", "ckvn%d" % kg], [PSN[b]])
        if h == 0:
            act(khT[hb][64:128, kcols], ps[b][64:128, :], AF.Copy, [PSN[b]], [khn])
        else:
            cp("dve", khT[hb][64:128, kcols], ps[b][64:128, :], [PSN[b]], [khn])

    def krot_piece(h):
        hb = h % 2
        cp("dve", khT[hb][0:32, :], krot[0:32, :], ["krot%d" % i_ for i_ in range(8)], ["khT%d" % hb])

    def v_piece(h, kq):
        hb = h % 2
        vhn = "Vh%d" % hb
        b = pj_bank()
        for j in range(8):
            kt = kq * 8 + j
            for kc in range(2):
                mm(ps[b][:, j * 64:(j + 1) * 64], ckvn[:, kc, kt * 128:(kt + 1) * 128], wvb[:, kc, h * 64:(h + 1) * 64],
                   kc == 0, kc == 1, ["wvb", "ckvn%d" % (kt // 4)], [PSN[b]])
        if h == 0:
            act(Vh[hb][:, kq * 8:(kq + 1) * 8, 0:64], ps[b][:, :].rearrange("p (a b) -> p a b", b=64), AF.Copy, [PSN[b]], [vhn])
        else:
            cp("dve", Vh[hb][:, kq * 8:(kq + 1) * 8, 0:64], ps[b][:, :].rearrange("p (a b) -> p a b", b=64), [PSN[b]], [vhn])

    def pool_mm_piece(g, tg):
        b = pj_bank()
        mm(ps[b][:, :], poolwb[:, g, :], pg[g % 2][:, tg * 512:(tg + 1) * 512], True, True,
           ["poolwb", "pg%d" % (g % 2)], [PSN[b]])
        ts("dve", mixT[:, g, tg * 512:(tg + 1) * 512], ps[b][:, :], vecs[:, V_PSC + g:V_PSC + g + 1], None,
           ALU.mult, None, [PSN[b], "vecs"], ["mixT%d" % g])

    def head_pieces(h):
        pcs = []
        for tg in range(4):
            pcs.append(lambda tg=tg: q_piece(h, tg))
        pcs.append(lambda: krot_piece(h))
        for kg in range(8):
            pcs.append(lambda kg=kg: k_piece(h, kg))
        for kq in range(4):
            pcs.append(lambda kq=kq: v_piece(h, kq))
        return pcs


    for tg in range(8):
        own = tg < 4
        if tg == 4:
            while rope_later:
                rope_tick()
        xb = xTb[tg % NXB]
        xbn = "xTb%d" % (tg % NXB)
        tcols = slice(tg * 512, (tg + 1) * 512)
        if tg + 2 < 8:
            load_x(tg + 2)
        qb = []
        if own:
            for j in range(3):
                b = next_bank()
                proj(b, winb_q, j * 128, 128, xb, xbn, "winb_q")
                act(sq[j][:, :], ps[b][:, :], AF.Square, [PSN[b]], ["sq%d" % id(sq[j])])
                qb.append(b)
        kvb = []
        for j in range(2):
            b = next_bank()
            proj(b, winb_kv, j * 128, 128, xb, xbn, "winb_kv")
            act(sq[3 + j][:, :], ps[b][:, :], AF.Square, [PSN[b]], ["sq%d" % id(sq[3 + j])])
            kvb.append(b)
        if own:
            norm_chain(qb, sq[0:3], 3, 1.0 / 384, V_QG, cqn, tcols, 0, "cqn%d" % tg)
        br = next_bank()
        proj(br, winb_kv, 192, 128, xb, xbn, "winb_kv")
        norm_chain(kvb, sq[3:5], 2, 1.0 / 256, V_KVG, ckvn, tcols, 1, "ckvn%d" % tg)
        cidx = "CS%d" % (tg // 2)
        tt("dve", rtA[0:32, :], ps[br][64:96, :], CS[0:32, tcols], ALU.mult, [PSN[br], cidx], ["rtA"])
        tt("dve", rtB[0:32, :], ps[br][96:128, :], CS[32:64, tcols], ALU.mult, [PSN[br], cidx], ["rtB"])
        tt("dve", krot[0:32, tcols], rtA[0:32, :], rtB[0:32, :], ALU.add, ["rtA", "rtB"], ["krot%d" % tg])
        if own:
            for g in range(4):
                b = next_bank()
                proj(b, winb_p, g * 128, 128, xb, xbn, "winb_p")
                act(hpool[:, g, 8 + tg * 512:8 + (tg + 1) * 512], ps[b][:, :], AF.Copy, [PSN[b]], ["hpool%d" % g])
        if tg == 3:
            for g in range(4):
                b = next_bank()
                for kc in range(8):
                    mm(ps[b][:, 0:16], winb_p[:, kc, g * 128:(g + 1) * 128], xhb[:, kc, :], kc == 0, kc == 7,
                       ["winb_p", "xhb"], [PSN[b]])
                act(hpool[:, g, 0:8], ps[b][:, 0:8], AF.Copy, [PSN[b]], ["hpool%d" % g])
                act(hpool[:, g, 2056:2064], ps[b][:, 8:16], AF.Copy, [PSN[b]], ["hpool%d" % g])


            while rope_later:
                rope_tick()
            RT_N = ["posi_a", "posi_b", "angf", "kf", "ki"]
            memset("pool", khT[0][32:64, :], 0.0, ["khT0z", "khT0", "qhT0", "qtA0", "qtB0"] + RT_N)
            memset("dve", Vh[0][:, :, 64:128], 1.0, ["Vh0", "winb_p", "winb_q", "qtA1", "qtB1"])
            q_piece(0, 0)
            k_piece(0, 0)
        if tg == 4:
            q_piece(0, 1); k_piece(0, 1); k_piece(0, 2); v_piece(0, 0)
        if tg == 5:
            q_piece(0, 2); k_piece(0, 3); k_piece(0, 4); v_piece(0, 1)
        if tg == 6:
            q_piece(0, 3); k_piece(0, 5); k_piece(0, 6); v_piece(0, 2)
        if tg == 7:
            k_piece(0, 7); v_piece(0, 3); krot_piece(0)

    P.barrier()
    tap("hpool", hpool[:, :, :], [128, 4, 2064], F32)
    tap("cqn", cqn[:, :, :], [128, 3, T], BF16)
    tap("ckvn", ckvn[:, :, :], [128, 2, S], BF16)
    tap("krot", krot[0:32, :], [32, S], BF16)
    tap("CS", CS[0:64, :], [64, S], F32)
    P.barrier()
    def pool_group(g):
        w = (2, 4, 8, 16)[g]
        U = hpool[:, g, :]
        hn = "hpool%d" % g
        bufs = [(tmpE[:, :], "tmpE")]
        if g >= 1:
            bufs.append((hpool[:, g - 1, :], "hpool%d" % (g - 1)))
        cur, curn = bufs[0]
        tt("pool", cur[:, 1:2064], U[:, 0:2063], U[:, 1:2064], ALU.add, [hn, "att_done%d" % g], [curn])
        lo, hi, sh = 1, 2064, 1
        nb = 1
        while sh * 2 < w:
            nxt, nxtn = bufs[nb % 2] if len(bufs) > 1 else bufs[0]
            lo2, hi2 = lo + sh, hi - sh
            tt("pool", nxt[:, lo2:hi2], cur[:, lo2 - sh:hi2 - sh], cur[:, lo2 + sh:hi2 + sh], ALU.add, [curn], [nxtn])
            cur, curn = nxt, nxtn
            lo, hi = lo2, hi2
            sh *= 2
            nb += 1
        assert lo <= 8 and hi >= 2056
        win = cur
        ts("pool", win[:, 8:2056], win[:, 8:2056], 1.0 / w, 0.0, ALU.mult, ALU.add, [curn], [curn])
        tt("pool", win[:, 8:16], win[:, 8:16], ratio_bc[:, g * 16:g * 16 + 8], ALU.mult, [curn, "ratio"], [curn])
        tt("pool", win[:, 2048:2056], win[:, 2048:2056], ratio_bc[:, g * 16 + 8:g * 16 + 16], ALU.mult, [curn, "ratio"], [curn])
        tt("pool", pg[g % 2][:, :], win[:, 8:2056], U[:, 8:2056], ALU.subtract, [curn, hn], ["pg%d" % (g % 2)])

    NSC = 4
    LAG = 3

    def attention(h, tg, pieces):
        hb = h % 2
        khn, qhn, vhn = "khT%d" % hb, "qhT%d" % hb, "Vh%d" % hb
        tcols = slice(tg * 512, (tg + 1) * 512)
        ob = 4 + (tg % 2)

        def pv(kt):
            mm(ps[ob][:, :], Vh[hb][:, kt, :], pT[kt % NPT][:, :], kt == 0, kt == 31,
               [vhn, "pT%d" % (kt % NPT)], [PSN[ob]])

        for kt in range(32):
            sb = kt % NSC
            mm(ps[sb][:, :], khT[hb][:, kt * 128:(kt + 1) * 128], qhT[hb][:, tcols], True, True, [khn, khn + "z", qhn], [PSN[sb]])
            act(pT[kt % NPT][:, :], ps[sb][:, :], AF.Exp, [PSN[sb]], ["pT%d" % (kt % NPT)], scale=SM_SCALE)
            if kt >= LAG:
                pv(kt - LAG)
            if kt % 7 == 3 and pieces:
                pieces.pop(0)()
        for kt in range(32 - LAG, 32):
            pv(kt)
        s_ = tg % 2
        P.op("dve", lambda e: e.reciprocal(out=rc[s_][0:64, :], in_=ps[ob][64:128, :]), [PSN[ob]], ["rc%d" % s_])
        po = (h % 2) * 64
        tt("dve", mixT[po:po + 64, 4 + h // 2, tcols], ps[ob][0:64, :], rc[s_][0:64, :], ALU.mult,
           [PSN[ob], "rc%d" % s_], ["mixT%d" % (4 + h // 2)] + (["att_done%d" % h] if tg == 0 else []))

    hp_off = m_whole
    wob_pre = nc.alloc_sbuf_tensor_at("wob_pre", [128, 8, D], BF16, offset=hp_off)
    lnbc1_pre = nc.alloc_sbuf_tensor_at("lnbc1_pre", [128, 2, D], F32, offset=hp_off + 16384)
    lnbc2_pre = nc.alloc_sbuf_tensor_at("lnbc2_pre", [128, 2, D], F32, offset=hp_off + 24576)
    HPR = ["hpool%d" % g_ for g_ in range(4)] + ["tmpE"]

    def prefetch_c():
        dma(wob_pre[:, :, :].rearrange("p (n a) c -> p n (a c)", n=4), wo_d[:, :, :], ["wo_d"] + HPR, HPR + ["wob"], eng="pool")
        for i_ in range(2):
            dma(lnbc1_pre[:, i_, :], bass.AP(lnp, i_ * D, [[0, 128], [1, D]]), ["lnp"] + HPR, HPR + ["lnbc1_%d" % i_])
            dma(lnbc2_pre[:, i_, :], bass.AP(lnp, (2 + i_) * D, [[0, 128], [1, D]]), ["lnp"] + HPR, HPR + ["lnbc2_%d" % i_])
        ts("pool", lnbc1_pre[:, 0, :], lnbc1_pre[:, 0, :], ALPHA, 0.0, ALU.mult, ALU.add, ["lnbc1_0"], ["lnbc1_0"])
        ts("pool", lnbc1_pre[:, 1, :], lnbc1_pre[:, 1, :], ALPHA, 0.0, ALU.mult, ALU.add, ["lnbc1_1"], ["lnbc1_1"])

    pj_nb[0] = 2
    init1 = [lambda: memset("pool", khT[1][32:64, :], 0.0, ["khT1z"]),
             lambda: memset("pool", Vh[1][:, :, 64:128], 1.0, ["Vh1"])]
    def att_sc(g, h, tg, kt):
        hb = h % 2
        khn, qhn = "khT%d" % hb, "qhT%d" % hb
        tcols = slice(tg * 512, (tg + 1) * 512)
        sb = g % NSC
        mm(ps[sb][:, :], khT[hb][:, kt * 128:(kt + 1) * 128], qhT[hb][:, tcols], True, True, [khn, khn + "z", qhn], [PSN[sb]])
        act(pT[g % NPT][:, :], ps[sb][:, :], AF.Exp, [PSN[sb]], ["pT%d" % (g % NPT)], scale=SM_SCALE)

    def att_pv(g, h, tg, kt):
        hb = h % 2
        vhn = "Vh%d" % hb
        tcols = slice(tg * 512, (tg + 1) * 512)
        ob = 4 + (tg % 2)
        mm(ps[ob][:, :], Vh[hb][:, kt, :], pT[g % NPT][:, :], kt == 0, kt == 31,
           [vhn, "pT%d" % (g % NPT)], [PSN[ob]])
        if kt == 31:
            s_ = tg % 2
            P.op("dve", lambda e: e.reciprocal(out=rc[s_][0:64, :], in_=ps[ob][64:128, :]), [PSN[ob]], ["rc%d" % s_])
            po = (h % 2) * 64
            tt("dve", mixT[po:po + 64, 4 + h // 2, tcols], ps[ob][0:64, :], rc[s_][0:64, :], ALU.mult,
               [PSN[ob], "rc%d" % s_], ["mixT%d" % (4 + h // 2)] + (["att_done%d" % h] if tg == 0 else []))
            if tg == 0 and h < 4:
                pool_group(h)

    pend = []
    g = 0
    pieces = []
    for h in range(8):
        if h == 6:
            prefetch_c()
        pieces = head_pieces(h + 1) if h + 1 < 8 else []
        if h == 0:
            pieces = init1 + pieces
        if 1 <= h <= 4:
            pieces = pieces + [lambda tg=tg, g_=h - 1: pool_mm_piece(g_, tg) for tg in range(4)]
        for tg in range(4):
            for kt in range(32):
                att_sc(g, h, tg, kt)
                pend.append((g, h, tg, kt))
                g += 1
                if len(pend) > LAG:
                    att_pv(*pend.pop(0))
                if kt % 7 == 3 and pieces:
                    pieces.pop(0)()
        while pieces:
            pieces.pop(0)()
    while pend:
        att_pv(*pend.pop(0))

    P.barrier()
    tap("mixT", mixT[:, :, :], [128, 8, T], BF16)
    P.barrier()
    A.release(m_whole)
    wob = wob_pre
    lnbc1 = lnbc1_pre
    lnbc2 = lnbc2_pre
    A.ptr = m_whole + 33024
    acc = A.tile([128, 16, D], F32, "acc")
    x1T = A.tile([128, 8, T], BF16, "x1T")
    NWG = 6
    NWD = 8
    wgub = [A.tile([128, 2, 8, 128], BF16, "wgub") for _ in range(NWG)]
    wdb = [A.tile([128, D], BF16, "wdb") for _ in range(6)]
    xt_off = A.mark()
    xt = [A.tile([128, D], F32, "xt") for _ in range(2)]
    wdb.append(nc.alloc_sbuf_tensor_at("wdb6", [128, D], BF16, offset=xt_off))
    wdb.append(nc.alloc_sbuf_tensor_at("wdb7", [128, D], BF16, offset=xt_off + 4096))
    WDN = ["wdb%d" % i_ for i_ in range(6)] + ["xt0", "xt1"]
    sg = [A.tile([128, 512], BF16, "sg") for _ in range(2)]
    NST = 4
    st = [A.tile([128, 12], F32, "st") for _ in range(NST)]
    mv = [A.tile([128, 2], F32, "mv") for _ in range(NST)]
    sm = [A.tile([128, 4], F32, "sm") for _ in range(NST)]
    def hact_ap(hb, j, cols):
        return mixT[:, hb * 4 + j, cols]

    def MR(c, tg):
        return "M%d_%d" % (c, tg)


    def load_gu(fc):
        s_ = fc % NWG
        dma(wgub[s_][:, :, :, :], wgu_d[fc], ["wgu_d"], ["wgb%d" % s_], eng="pool")

    def load_d(fc):
        s_ = fc % NWD
        dma(wdb[s_][:, :], wd_d[fc], ["wd_d"], [WDN[s_]], eng="pool")

    for fc in range(NWG):
        load_gu(fc)
    for fc in range(6):
        load_d(fc)

    def ln_stats_a(src, srcn, i):
        st_i, mv_i, sm_i = st[i], mv[i], sm[i]
        P.op("dve", lambda e: e.bn_stats(out=st_i[:, 0:6], in_=src[:, 0:512]), [srcn], ["st%d" % i])
        P.op("dve", lambda e: e.bn_stats(out=st_i[:, 6:12], in_=src[:, 512:1024]), [srcn], ["st%d" % i])
        P.op("dve", lambda e: e.bn_aggr(out=mv_i[:, 0:2], in_=st_i[:, 0:12]), ["st%d" % i], ["mv%d" % i])
        ts("pool", sm_i[:, 2:3], mv_i[:, 1:2], LN_EPS, 1.0, ALU.add, ALU.mult, ["mv%d" % i], ["sm%d" % i])
        tt("pool", sm_i[:, 0:1], sm_i[:, 2:3], neghalf[:, 0:1], ALU.pow, ["sm%d" % i, "neghalf"], ["sm%d" % i])

    def ln_stats_b(i):
        mv_i, sm_i = mv[i], sm[i]
        ts("dve", sm_i[:, 1:2], mv_i[:, 0:1], sm_i[:, 0:1], -1.0, ALU.mult, ALU.mult, ["mv%d" % i, "sm%d" % i], ["sm%d" % i])

    def normalize(out, in_, i, reads, writes):
        sm_i = sm[i]
        P.op("act", lambda e: e.activation(out=out, in_=in_, func=AF.Identity, scale=sm_i[:, 0:1], bias=sm_i[:, 1:2]),
             reads, writes)

    groups = [[0, 1, 2, 3], [4, 5, 6, 7], [8, 9, 10, 11], [12, 13], [14, 15, 16, 17], [18, 19, 20, 21]]
    gu_rr = [0]
    sg_rr = [0]
    dn_rr = [0]

    def gate_up(fg, j, tg):
        fc = groups[fg][j]
        s = fc % NWG
        hb = fg % 2
        tcols = slice(tg * 512, (tg + 1) * 512)
        pair = gu_rr[0] % 2
        gu_rr[0] += 1
        base_b = 4 if fg == 0 else 0
        bg, bu = base_b + 2 * pair, base_b + 2 * pair + 1
        for kc in range(8):
            mm(ps[bg][:, :], wgub[s][:, 0, kc, :], x1T[:, kc, tcols], kc == 0, kc == 7,
               ["wgb%d" % s, "x1T%d_%d" % (tg, kc)], [PSN[bg]])
        for kc in range(8):
            mm(ps[bu][:, :], wgub[s][:, 1, kc, :], x1T[:, kc, tcols], kc == 0, kc == 7,
               ["wgb%d" % s, "x1T%d_%d" % (tg, kc)], [PSN[bu]])
        si = sg_rr[0] % 2
        sg_rr[0] += 1
        act(sg[si][:, :], ps[bg][:, :], AF.Silu, [PSN[bg]], ["sg%d" % si])
        tt("dve", hact_ap(hb, j, tcols), ps[bu][:, :], sg[si][:, :], ALU.mult, [PSN[bu], "sg%d" % si], [MR(hb * 4 + j, tg)])

    def down(fg, t_, half):
        hb = fg % 2
        fcs = groups[fg]
        b = 4 + dn_rr[0] % 4
        dn_rr[0] += 1
        tok = slice(t_ * 128, (t_ + 1) * 128)
        hc = slice(half * 512, (half + 1) * 512)
        for j, fc in enumerate(fcs):
            mm(ps[b][:, :], hact_ap(hb, j, tok), wdb[fc % NWD][:, hc], j == 0, j == len(fcs) - 1,
               [MR(hb * 4 + j, t_ // 4), WDN[fc % NWD]], [PSN[b]])
        tt("dve", acc[:, t_, hc], ps[b][:, :], acc[:, t_, hc], ALU.add, [PSN[b], "acc%d_%d" % (t_, half)], ["acc%d_%d" % (t_, half)])

    def gu_items(fg):
        return [(fg, j, tg) for tg in range(4) for j in range(len(groups[fg]))]

    def dn_items(fg):
        return [(fg, t_, half) for t_ in range(16) for half in range(2)]

    def c1_load(t_):
        i = t_ % 2
        dma(xt[i][:, :], xown[t_ * 128:(t_ + 1) * 128, :], ["xown"], ["xt%d" % i])

    def c1_mix(t_):
        i = t_ % 2
        tok = slice(t_ * 128, (t_ + 1) * 128)
        for half in range(2):
            b = half
            hc = slice(half * 512, (half + 1) * 512)
            for kc in range(8):
                mm(ps[b][:, :], mixT[:, kc, tok], wob[:, kc, hc], kc == 0, kc == 7,
                   [MR(kc, t_ // 4), "wob"], [PSN[b]])
            stt(acc[:, t_, hc], xt[i][:, hc], ALPHA, ps[b][:, :],
                ALU.mult, ALU.add, ["xt%d" % i, PSN[b]], ["acc%d_%d" % (t_, half)])
        j_ = t_ % NST
        src = acc[:, t_, :]
        st_i, mv_i, sm_i = st[j_], mv[j_], sm[j_]
        P.op("dve", lambda e: e.bn_stats(out=st_i[:, 0:6], in_=src[:, 0:512]), ["acc%d_0" % t_], ["st%d" % j_])
        P.op("dve", lambda e: e.bn_stats(out=st_i[:, 6:12], in_=src[:, 512:1024]), ["acc%d_1" % t_], ["st%d" % j_])
        P.op("dve", lambda e: e.bn_aggr(out=mv_i[:, 0:2], in_=st_i[:, 0:12]), ["st%d" % j_], ["mv%d" % j_])
        ts("pool", sm_i[:, 2:3], mv_i[:, 1:2], LN_EPS, 1.0, ALU.add, ALU.mult, ["mv%d" % j_], ["sm%d" % j_])
        tt("pool", sm_i[:, 0:1], sm_i[:, 2:3], neghalf[:, 0:1], ALU.pow, ["sm%d" % j_, "neghalf"], ["sm%d" % j_])

    def c1_norm(t_):
        an2 = ["acc%d_0" % t_, "acc%d_1" % t_]
        ln_stats_b(t_ % NST)
        normalize(acc[:, t_, :], acc[:, t_, :], t_ % NST, an2 + ["sm%d" % (t_ % NST)], an2)

    def c1_init(t_):
        an2 = ["acc%d_0" % t_, "acc%d_1" % t_]
        tt("pool", acc[:, t_, :], acc[:, t_, :], lnbc1[:, 0, :], ALU.mult, an2 + ["lnbc1_0"], an2)
        tt("pool", acc[:, t_, :], acc[:, t_, :], lnbc1[:, 1, :], ALU.add, an2 + ["lnbc1_1"], an2)

    def c1_tr(t_):
        i = t_ % 2
        tok = slice(t_ * 128, (t_ + 1) * 128)
        for q4 in range(2):
            b = 2 + q4
            for j in range(4):
                kc = q4 * 4 + j
                P.op("pe", lambda e, o_=ps[b][:, j * 128:(j + 1) * 128], i_=acc[:, t_, kc * 128:(kc + 1) * 128]:
                     e.transpose(out=o_, in_=i_, identity=ident[:, :]), ["acc%d_%d" % (t_, kc // 4), "ident"], [PSN[b]])
            for j in range(4):
                kc = q4 * 4 + j
                if q4 == 0:
                    ts("dve", x1T[:, kc, tok], ps[b][:, j * 128:(j + 1) * 128], vecs[:, V_G1 + kc:V_G1 + kc + 1],
                       vecs[:, V_B1 + kc:V_B1 + kc + 1], ALU.mult, ALU.add, [PSN[b], "vecs"], ["x1T%d_%d" % (t_ // 4, kc)])
                else:
                    P.op("act", lambda e, o_=x1T[:, kc, tok], i_=ps[b][:, j * 128:(j + 1) * 128],
                         s_=vecs[:, V_G1 + kc:V_G1 + kc + 1], b_=vecs[:, V_B1 + kc:V_B1 + kc + 1]:
                         e.activation(out=o_, in_=i_, func=AF.Identity, scale=s_, bias=b_),
                         [PSN[b], "vecs"], ["x1T%d_%d" % (t_ // 4, kc)])

    ready_gu = []
    gu0 = gu_items(0)
    c1_load(0)
    c1_load(1)
    for k in range(16 + 2):
        if 1 <= k <= 16:
            c1_norm(k - 1)
        if k < 16:
            c1_mix(k)
            if k + 2 < 16:
                c1_load(k + 2)
        if 2 <= k <= 17:
            t_ = k - 2
            c1_tr(t_)
            if t_ % 4 == 3:
                tg = t_ // 4
                ready_gu += [it for it in gu0 if it[2] == tg]
        if 3 <= k:
            c1_init(k - 3)
        for _ in range(2):
            if ready_gu:
                gate_up(*ready_gu.pop(0))
    c1_init(15)
    while ready_gu:
        gate_up(*ready_gu.pop(0))
    for fc in groups[0]:
        if fc + NWG < NFC:
            load_gu(fc + NWG)
    load_d(6)
    load_d(7)

    tap("x1T", x1T[:, :, :], [128, 8, T], BF16, reads=["x1T%d_%d" % (g_, k_) for g_ in range(4) for k_ in range(8)])

    def ln2_a(t_):
        i = t_ % NST
        src = acc[:, t_, :]
        st_i, mv_i, sm_i = st[i], mv[i], sm[i]
        P.op("dve", lambda e: e.bn_stats(out=st_i[:, 0:6], in_=src[:, 0:512]), ["acc%d_0" % t_], ["st%d" % i])
        P.op("dve", lambda e: e.bn_stats(out=st_i[:, 6:12], in_=src[:, 512:1024]), ["acc%d_1" % t_], ["st%d" % i])
        P.op("dve", lambda e: e.bn_aggr(out=mv_i[:, 0:2], in_=st_i[:, 0:12]), ["st%d" % i], ["mv%d" % i])
        ts("pool", sm_i[:, 2:3], mv_i[:, 1:2], LN_EPS, 1.0, ALU.add, ALU.mult, ["mv%d" % i], ["sm%d" % i])
        tt("pool", sm_i[:, 0:1], sm_i[:, 2:3], neghalf[:, 0:1], ALU.pow, ["sm%d" % i, "neghalf"], ["sm%d" % i])

    def ln2_b(t_):
        an2 = ["acc%d_0" % t_, "acc%d_1" % t_]
        ln_stats_b(t_ % NST)
        normalize(acc[:, t_, :], acc[:, t_, :], t_ % NST, an2 + ["sm%d" % (t_ % NST)], an2)

    def ln2_c(t_):
        an2 = ["acc%d_0" % t_, "acc%d_1" % t_]
        tok = slice(t_ * 128, (t_ + 1) * 128)
        tt("dve", acc[:, t_, :], acc[:, t_, :], lnbc2[:, 0, :], ALU.mult, an2 + ["lnbc2_0"], an2)
        tt("dve" if t_ >= 14 else "pool", acc[:, t_, :], acc[:, t_, :], lnbc2[:, 1, :], ALU.add, an2 + ["lnbc2_1"], an2)
        dma(y[tok, :], acc[:, t_, :], an2, ["y%d" % t_])

    def down_final(t_, half):
        b = 4 + dn_rr[0] % 4
        dn_rr[0] += 1
        tok = slice(t_ * 128, (t_ + 1) * 128)
        hc = slice(half * 512, (half + 1) * 512)
        items = [(4, j, fc) for j, fc in enumerate(groups[4])] + [(5, j, fc) for j, fc in enumerate(groups[5])]
        for n_, (fg_, j, fc) in enumerate(items):
            hb = fg_ % 2
            mm(ps[b][:, :], hact_ap(hb, j, tok), wdb[fc % NWD][:, hc], n_ == 0, n_ == len(items) - 1,
               [MR(hb * 4 + j, t_ // 4), WDN[fc % NWD]], [PSN[b]])
        tt("dve", acc[:, t_, hc], ps[b][:, :], acc[:, t_, hc], ALU.add, [PSN[b], "acc%d_%d" % (t_, half)], ["acc%d_%d" % (t_, half)])

    for fg in range(4):
        gus = gu_items(fg + 1)
        dns = dn_items(fg)
        gi = 0
        for _ in range(3):
            if gi < len(gus):
                gate_up(*gus[gi])
                gi += 1
        for di, d_ in enumerate(dns):
            down(*d_)
            if gi < len(gus):
                gate_up(*gus[gi])
                gi += 1
        while gi < len(gus):
            gate_up(*gus[gi])
            gi += 1
        for fc in groups[fg + 1]:
            if fc + NWG < NFC:
                load_gu(fc + NWG)
        for fc in groups[fg]:
            if fc + NWD < NFC:
                load_d(fc + NWD)
    gu5 = gu_items(5)
    gi5 = 0
    for _ in range(4):
        gate_up(*gu5[gi5])
        gi5 += 1
    for t_ in range(16):
        for half in range(2):
            down_final(t_, half)
        ln2_a(t_)
        if t_ >= 1:
            ln2_b(t_ - 1)
        if t_ >= 2:
            ln2_c(t_ - 2)
        if gi5 < len(gu5):
            gate_up(*gu5[gi5])
            gi5 += 1
    ln2_b(15)
    ln2_c(14)
    ln2_c(15)

    P.op("sp", lambda e: e.nop(), ["y%d" % t_ for t_ in range(16)] + ["dbg_" + k for k in taps], ())
    P.emit()
    return nc


def _host_layouts(inp):
    x = np.asarray(inp["x"], dtype=np.float32)
    positions = np.asarray(inp["positions"]).astype(np.int32)
    w_in = np.asarray(inp["w_in"], dtype=np.float32)[0]
    rope = w_in[:, 1152:1184]
    w_in_ext = np.concatenate([w_in, rope[:, 16:32], rope[:, 0:16]], axis=1)
    w_in_l = w_in_ext.reshape(8, 128, WIN_COLS).transpose(1, 0, 2)
    winq_l = np.ascontiguousarray(w_in_l[:, :, 512:896]).reshape(128, 2, 1536)
    winkv_l = np.ascontiguousarray(w_in_l[:, :, 896:WIN_COLS]).reshape(128, 2, 1280)
    winp_l = np.ascontiguousarray(w_in_l[:, :, 0:512]).reshape(128, 2, 2048)
    pool_w_l = np.ascontiguousarray(np.asarray(inp["pool_w"], dtype=np.float32)[0].transpose(1, 0, 2))
    wq = np.asarray(inp["w_q_up"], dtype=np.float32)[0].reshape(384, 8, 96)
    wq_h = np.concatenate([wq[:, :, 64:96], wq[:, :, 80:96], wq[:, :, 64:80], wq[:, :, 0:64]], axis=2)
    wq_l = np.ascontiguousarray(wq_h.reshape(3, 128, 1024).transpose(1, 0, 2)).reshape(128, 2, 1536)
    wk4 = np.asarray(inp["w_k_up"], dtype=np.float32)[0].reshape(2, 128, 8, 64).transpose(1, 0, 2, 3)
    wk_pad = np.zeros((128, 2, 8, 128), np.float32)
    wk_pad[:, :, :, 64:128] = wk4
    wk_l = np.ascontiguousarray(wk_pad).reshape(128, 1, 2048)
    wv_l = np.ascontiguousarray(np.asarray(inp["w_v_up"], dtype=np.float32)[0].reshape(2, 128, 512).transpose(1, 0, 2))
    wo_l = np.ascontiguousarray(np.asarray(inp["w_o"], dtype=np.float32)[0].reshape(8, 128, 1024).transpose(1, 0, 2)).reshape(128, 4, 2048)
    wg_l = np.asarray(inp["w_gate"], dtype=np.float32)[0].reshape(8, 128, NFC, 128).transpose(2, 1, 0, 3)
    wu_l = np.asarray(inp["w_up"], dtype=np.float32)[0].reshape(8, 128, NFC, 128).transpose(2, 1, 0, 3)
    wgu_l = np.ascontiguousarray(np.stack([wg_l, wu_l], axis=2))
    wd_l = np.ascontiguousarray(np.asarray(inp["w_down"], dtype=np.float32)[0].reshape(NFC, 128, 1024))
    lnp = np.ascontiguousarray(np.stack([np.asarray(inp[k], dtype=np.float32)[0] for k in ("ln1_g", "ln1_b", "ln2_g", "ln2_b")]))
    vecs = np.zeros((128, NVEC), np.float32)
    vecs[:, V_PSC:V_PSC + 4] = np.asarray(inp["pool_scale"], dtype=np.float32)[0].reshape(4, 128).T
    vecs[:, V_QG:V_QG + 3] = np.asarray(inp["q_norm_g"], dtype=np.float32)[0].reshape(3, 128).T
    vecs[:, V_KVG:V_KVG + 2] = np.asarray(inp["kv_norm_g"], dtype=np.float32)[0].reshape(2, 128).T
    vecs[:, V_G1:V_G1 + 8] = np.asarray(inp["ln1_g"], dtype=np.float32)[0].reshape(8, 128).T
    vecs[:, V_B1:V_B1 + 8] = np.asarray(inp["ln1_b"], dtype=np.float32)[0].reshape(8, 128).T
    inv64 = 1.0 / (10000.0 ** (np.arange(0, 32, 2, dtype=np.float64) / 32.0))
    a32 = inv64.astype(np.float32)
    a_hi = (a32.view(np.uint32) & np.uint32(0xFFFFF000)).view(np.float32)
    a_lo = (inv64 - a_hi.astype(np.float64)).astype(np.float32)
    a = np.zeros(128, np.float32)
    a2 = np.zeros(128, np.float32)
    b = np.zeros(128, np.float32)
    for r0 in (0, 16, 32, 48):
        a[r0:r0 + 16] = a_hi
        a2[r0:r0 + 16] = a_lo
    b[0:32] = np.float32(math.pi / 2)
    b[32:48] = np.float32(math.pi)
    a[64:128] = a[0:64]
    a2[64:128] = a2[0:64]
    b[64:128] = b[0:64]
    vecs[:, V_RA2] = a2
    vecs[:, V_RA] = a
    vecs[:, V_RB] = b
    shared = dict(vecs=vecs, winq_l=winq_l, winkv_l=winkv_l, winp_l=winp_l, pool_w_l=pool_w_l, wq_l=wq_l, wk_l=wk_l, wv_l=wv_l, wo_l=wo_l,
                  wgu_l=wgu_l, wd_l=wd_l, lnp=lnp)
    in_maps = []
    for c in range(8):
        b_, half = c // 2, c % 2
        o0, o1 = half * T, (half + 1) * T
        r0, r1 = (1 - half) * T, (2 - half) * T
        xo = x[b_, o0:o1]
        xr = x[b_, r0:r1]
        xT = np.concatenate([xo, xr], axis=0).T
        xT_l = np.ascontiguousarray(xT.reshape(8, 128, 8, 512).transpose(2, 1, 0, 3)).reshape(8, 128, 2, 2048)
        xh = np.zeros((16, D), np.float32)
        if o0 >= 8:
            xh[0:8] = x[b_, o0 - 8:o0]
        if o1 + 8 <= S:
            xh[8:16] = x[b_, o1:o1 + 8]
        pos = np.concatenate([positions[b_, o0:o1], positions[b_, r0:r1]])[None, :].astype(np.int32)
        ratio = np.ones((4, 16), np.float32)
        for g, w in enumerate((2, 4, 8, 16)):
            for k in range(16):
                gi = o0 + k if k < 8 else o1 - 16 + k
                lo = max(gi - w // 2, 0)
                hi = min(gi + w - w // 2, S)
                ratio[g, k] = np.float32(w) / np.float32(hi - lo)
        m = dict(shared)
        m.update(xT_l=xT_l, xown=np.ascontiguousarray(xo),
                 xh_l=np.ascontiguousarray(xh.T.reshape(8, 128, 16).transpose(1, 0, 2)).reshape(128, 128), pos=np.ascontiguousarray(pos),
                 ratio=np.ascontiguousarray(ratio.reshape(1, 64)))
        in_maps.append(m)
    return in_maps


_NC_CACHE = {}


def kernel(**inputs):
    in_maps = _host_layouts(inputs)
    if "nc" not in _NC_CACHE:
        _NC_CACHE["nc"] = build_nc()
    nc = _NC_CACHE["nc"]
    res = run_bass_kernel_spmd(nc, in_maps, core_ids=list(range(8)))
    out = np.zeros((4, S, D), np.float32)
    for c in range(8):
        b_, half = c // 2, c % 2
        out[b_, half * T:(half + 1) * T] = np.asarray(res.results[c]["y"], dtype=np.float32)
    return out
```

```python
import math
import numpy as np
import concourse.bass as bass
import concourse.mybir as mybir
from concourse.bass_utils import run_bass_kernel_spmd

F32 = mybir.dt.float32
BF16 = mybir.dt.bfloat16
I32 = mybir.dt.int32
ALU = mybir.AluOpType
AF = mybir.ActivationFunctionType

D = 1024
S = 4096
T = 2048
H = 8
FF = 2816
NFC = 22
WIN_COLS = 1216
ALPHA = 2.0 ** 0.25
LN_EPS = 1e-5
RMS_EPS = 1e-6
SM_SCALE = 96.0 ** -0.5
TWO_PI = 2.0 * math.pi
PI_LO = 3.1415925
CW1 = 6.28125
CW2 = TWO_PI - CW1
NVEC = 28
V_PSC, V_QG, V_KVG, V_G1, V_B1, V_RA, V_RB, V_RA2 = 0, 4, 7, 9, 17, 25, 26, 27

ENGS = ("pe", "act", "dve", "pool", "sp")
SAME_ENGINE_ALL = True


class Op:
    __slots__ = ("eng", "seq", "fn", "waits", "needs_inc", "inc_val", "clock", "is_dma")

    def __init__(self, eng, seq, fn, is_dma=False):
        self.eng = eng
        self.seq = seq
        self.fn = fn
        self.waits = []
        self.needs_inc = False
        self.inc_val = None
        self.clock = None
        self.is_dma = is_dma


class Prog:
    def __init__(self, nc, n_vq=8):
        self.nc = nc
        self.n_vq = n_vq
        self.vqs = ["vq%d" % i for i in range(n_vq)]
        self.streams = list(ENGS) + self.vqs
        self.issue = {e: [] for e in ENGS}
        self.count = {s: 0 for s in self.streams}
        self.clock = {e: {s: 0 for s in self.streams} for e in ENGS}
        self.res_w = {}
        self.res_r = {}
        self.vq_next = 0
        self.vq_last = {q: None for q in self.vqs}
        self.sw_streams = []

    def _deps(self, reads, writes):
        deps = []
        for r in reads:
            w = self.res_w.get(r)
            if w is not None:
                deps.append((w, "raw"))
        for w_ in writes:
            w = self.res_w.get(w_)
            if w is not None:
                deps.append((w, "waw"))
            for rd in self.res_r.get(w_, ()):
                deps.append((rd, "war"))
        return deps

    def _add_waits(self, op, issue_eng, deps):
        clk = self.clock[issue_eng]
        best = {}
        for d, kind in deps:
            if d.eng == issue_eng and not d.is_dma:
                if issue_eng == "pe" or (kind != "raw" and not SAME_ENGINE_ALL):
                    continue
            if clk.get(d.eng, 0) >= d.seq:
                continue
            if d.eng not in best or best[d.eng].seq < d.seq:
                best[d.eng] = d
        for d in best.values():
            if clk.get(d.eng, 0) >= d.seq:
                continue
            op.waits.append(d)
            d.needs_inc = True
            clk[d.eng] = max(clk.get(d.eng, 0), d.seq)
            for s, v in d.clock.items():
                if clk.get(s, 0) < v:
                    clk[s] = v

    def _commit(self, op, reads, writes):
        for r in reads:
            self.res_r.setdefault(r, []).append(op)
        for w in writes:
            self.res_w[w] = op
            self.res_r[w] = []

    def op(self, eng, fn, reads=(), writes=()):
        self.count[eng] += 1
        o = Op(eng, self.count[eng], fn)
        self._add_waits(o, eng, self._deps(reads, writes))
        o.clock = dict(self.clock[eng])
        self.issue[eng].append(o)
        self._commit(o, reads, writes)
        return o

    def dma(self, fn, reads=(), writes=(), eng="sp"):
        if eng == "pool":
            q = "sw%d" % len(self.sw_streams)
            self.sw_streams.append(q)
            self.streams.append(q)
            self.count[q] = 0
            self.vq_last[q] = None
            for e in ENGS:
                self.clock[e][q] = 0
        else:
            q = self.vqs[self.vq_next]
            self.vq_next = (self.vq_next + 1) % self.n_vq
        self.count[q] += 1
        o = Op(q, self.count[q], fn, is_dma=True)
        deps = self._deps(reads, writes)
        prev = self.vq_last[q]
        if prev is not None:
            deps.append((prev, "raw"))
        self._add_waits(o, eng, deps)
        o.clock = dict(self.clock[eng])
        self.vq_last[q] = o
        self.issue[eng].append(o)
        self._commit(o, reads, writes)
        return o

    def barrier(self):
        last = {}
        for e in ENGS:
            for o in self.issue[e]:
                if o.fn is None:
                    continue
                if o.eng not in last or last[o.eng].seq < o.seq:
                    last[o.eng] = o
        for e in ENGS:
            self.count[e] += 1
            o = Op(e, self.count[e], None)
            deps = [(d, "raw") for d in last.values()]
            self._add_waits(o, e, deps)
            o.clock = dict(self.clock[e])
            self.issue[e].append(o)
        self.res_w = {}
        self.res_r = {}

    def emit(self):
        nc = self.nc
        per_stream = {s: [] for s in self.streams}
        for e in ENGS:
            for o in self.issue[e]:
                per_stream[o.eng].append(o)
        sems = {s: nc.alloc_semaphore("sem_" + s) for s in self.streams
                if any(o.needs_inc for o in per_stream[s])}
        for s, lst in per_stream.items():
            lst.sort(key=lambda o: o.seq)
            c = 0
            for o in lst:
                if o.needs_inc:
                    c += 16 if o.is_dma else 1
                    o.inc_val = c
        prog = self

        def run(engname):
            def body(eng):
                for o in prog.issue[engname]:
                    for d in o.waits:
                        eng.wait_ge(sems[d.eng], d.inc_val)
                    if o.fn is None:
                        if o.needs_inc:
                            eng.nop().then_inc(sems[o.eng], 1)
                        continue
                    ins = o.fn(eng)
                    if o.needs_inc:
                        ins.then_inc(sems[o.eng], 16 if o.is_dma else 1)
            return body

        with nc.Block() as block:
            block.tensor(run("pe"))
            block.scalar(run("act"))
            block.vector(run("dve"))
            block.gpsimd(run("pool"))
            block.sync(run("sp"))


class Arena:
    def __init__(self, nc, base, top):
        self.nc = nc
        self.base = base
        self.top = top
        self.ptr = base
        self.n = 0

    def mark(self):
        return self.ptr

    def release(self, m):
        self.ptr = m

    def tile(self, shape, dtype, name=None):
        nbytes = int(np.prod(shape[1:])) * mybir.dt.size(dtype)
        nbytes = (nbytes + 31) // 32 * 32
        off = self.ptr
        assert off + nbytes <= self.top, ("SBUF overflow", name, off + nbytes - self.top)
        self.ptr += nbytes
        self.n += 1
        return self.nc.alloc_sbuf_tensor_at("%s_%d" % (name or "t", self.n), list(shape), dtype, offset=off)


def build_nc(debug=False):
    nc = bass.Bass("TRN2", target_bir_lowering=False)
    taps = {}

    def tap(name, tile_ap, shape, dt, reads=()):
        if not debug:
            return
        o = nc.dram_tensor("dbg_" + name, list(shape), dt, kind="ExternalOutput").ap()
        P.dma(lambda e: e.dma_start(out=o, in_=tile_ap), list(reads), ["dbg_" + name])
        taps[name] = 1

    def din(name, shape, dt=F32):
        return nc.dram_tensor(name, list(shape), dt, kind="ExternalInput").ap()

    xT_l = din("xT_l", [8, 128, 2, 2048])
    xown = din("xown", [T, D])
    xh_l = din("xh_l", [128, 128])
    pos = nc.dram_tensor("pos", [1, S], I32, kind="ExternalInput")
    ratio = nc.dram_tensor("ratio", [1, 64], F32, kind="ExternalInput")
    vecs_d = din("vecs", [128, NVEC])
    winq_d = din("winq_l", [128, 2, 1536])
    winkv_d = din("winkv_l", [128, 2, 1280])
    winp_d = din("winp_l", [128, 2, 2048])
    poolw_d = din("pool_w_l", [128, 4, 128])
    wq_d = din("wq_l", [128, 2, 1536])
    wk_d = din("wk_l", [128, 1, 2048])
    wv_d = din("wv_l", [128, 2, 512])
    wo_d = din("wo_l", [128, 4, 2048])
    wgu_d = din("wgu_l", [NFC, 128, 2, 8, 128])
    wd_d = din("wd_l", [NFC, 128, 1024])
    lnp = nc.dram_tensor("lnp", [4, D], F32, kind="ExternalInput")
    y = nc.dram_tensor("y", [T, D], F32, kind="ExternalOutput").ap()

    P = Prog(nc)
    base = (nc.sbuf_base + 63) // 64 * 64
    A = Arena(nc, base, nc.sbuf_top)
    ps = [nc.alloc_psum_tensor("psb%d" % i, [128, 512], F32) for i in range(8)]
    PSN = ["ps%d" % i for i in range(8)]

    def mm(out, lhsT, rhs, start, stop, reads, writes):
        P.op("pe", lambda e: e.matmul(out, lhsT=lhsT, rhs=rhs, start=start, stop=stop), reads, writes)

    def act(out, in_, func, reads, writes, scale=1.0, bias=0.0):
        P.op("act", lambda e: e.activation(out=out, in_=in_, func=func, scale=scale, bias=bias), reads, writes)

    def tt(eng, out, in0, in1, op, reads, writes):
        P.op(eng, lambda e: e.tensor_tensor(out=out, in0=in0, in1=in1, op=op), reads, writes)

    def ts(eng, out, in0, s1, s2, op0, op1, reads, writes):
        if s2 is None:
            P.op(eng, lambda e: e.tensor_scalar(out=out, in0=in0, scalar1=s1, scalar2=None, op0=op0), reads, writes)
        else:
            P.op(eng, lambda e: e.tensor_scalar(out=out, in0=in0, scalar1=s1, scalar2=s2, op0=op0, op1=op1), reads, writes)

    def stt(out, in0, scalar, in1, op0, op1, reads, writes):
        P.op("dve", lambda e: e.scalar_tensor_tensor(out=out, in0=in0, scalar=scalar, in1=in1, op0=op0, op1=op1), reads, writes)

    def cp(eng, out, in_, reads, writes):
        P.op(eng, lambda e: e.tensor_copy(out=out, in_=in_), reads, writes)

    def memset(eng, ap, val, writes):
        P.op(eng, lambda e: e.memset(ap, val), (), writes)

    def dma(out, in_, reads, writes, eng="sp"):
        P.dma(lambda e: e.dma_start(out=out, in_=in_), reads, writes, eng=eng)

    vecs = A.tile([128, NVEC], F32, "vecs")
    ident = A.tile([128, 128], F32, "ident")
    ones = A.tile([128, 128], BF16, "ones")
    neghalf = A.tile([128, 1], F32, "neghalf")
    mixT = A.tile([128, 8, T], BF16, "mixT")
    m_whole = A.mark()

    dma(vecs[:, :], vecs_d[:, :], ["vecs_d"], ["vecs"])

    hpool = A.tile([128, 4, 2064], F32, "hpool")
    cqn = A.tile([128, 3, T], BF16, "cqn")
    ckvn = A.tile([128, 2, S], BF16, "ckvn")
    krot = A.tile([128, S], BF16, "krot")
    CS = A.tile([128, S], F32, "CS")
    wqb = A.tile([128, 3, 1024], BF16, "wqb")
    wkb = A.tile([128, 2, 1024], BF16, "wkb")
    wvb = A.tile([128, 2, 512], BF16, "wvb")
    poolwb = A.tile([128, 4, 128], BF16, "poolwb")
    ratio_bc = A.tile([128, 64], F32, "ratio")
    m_ab = A.mark()

    NXB = 3
    xTb = [A.tile([128, 8, 512], BF16, "xTb") for _ in range(NXB)]
    O_WQ = A.mark() - m_ab
    winb_q = A.tile([128, 8, 384], BF16, "winb_q")
    winb_kv = A.tile([128, 8, 320], BF16, "winb_kv")
    O_WP = A.mark() - m_ab
    winb_p = A.tile([128, 8, 512], BF16, "winb_p")
    xhb = A.tile([128, 8, 16], BF16, "xhb")
    sq = [A.tile([128, 512], BF16, "sq") for _ in range(5)]
    rstd = [A.tile([128, 512], F32, "rstd") for _ in range(2)]
    lnt = [A.tile([128, 512], F32, "lnt") for _ in range(2)]
    rtA = A.tile([128, 512], F32, "rtA")
    rtB = A.tile([128, 512], F32, "rtB")
    O_RT = A.mark() - m_ab
    posi = A.tile([128, 1024], I32, "posi")
    angf = A.tile([128, 1024], F32, "angf")
    kf = A.tile([128, 1024], F32, "kf")
    ki = A.tile([128, 1024], I32, "ki")
    assert (O_WQ, O_WP, O_RT) == (24576, 35840, 61696) and A.mark() - m_ab == 78080

    def big(ap3, n):
        return ap3.rearrange("p (n a) b -> p n (a b)", n=n)

    def load_x(tg):
        b = tg % NXB
        dma(big(xTb[b][:, :, :], 2), xT_l[tg], ["xT"], ["xTb%d" % b], eng="pool")

    vx0, vq0 = big(xTb[0][:, :, :], 2), big(winb_q[:, :, :], 2)
    dma(vx0[:, 0:1, :], xT_l[0][:, 0:1, :], ["xT"], ["xTb0"], eng="pool")
    dma(vq0[:, 0:1, :], winq_d[:, 0:1, :], ["win_d"], ["winb_q"], eng="pool")
    dma(vx0[:, 1:2, :], xT_l[0][:, 1:2, :], ["xT"], ["xTb0h"], eng="pool")
    dma(vq0[:, 1:2, :], winq_d[:, 1:2, :], ["win_d"], ["winb_qh"], eng="pool")
    dma(big(winb_kv[:, :, :], 2), winkv_d[:, :, :], ["win_d"], ["winb_kv"], eng="pool")
    dma(big(winb_p[:, :, :], 2), winp_d[:, :, :], ["win_d"], ["winb_p"], eng="pool")
    dma(xhb[:, :, :].rearrange("p k s -> p (k s)"), xh_l[:, :], ["xhT"], ["xhb"], eng="pool")
    load_x(1)
    memset("pool", ones[:, :], 1.0, ["ones"])
    memset("pool", neghalf[:, :], -0.5, ["neghalf"])
    memset("pool", ident[:, :], 1.0, ["ident"])
    P.op("pool", lambda e: e.affine_select(out=ident[:, :], in_=ident[:, :], pattern=[[-1, 128]],
                                           compare_op=ALU.is_equal, fill=0.0, base=0, channel_multiplier=1),
         ["ident"], ["ident"])
    dma(wqb[:, :, :].rearrange("p k (n c) -> p (k n) c", n=2).rearrange("p (n a) c -> p n (a c)", n=2), wq_d[:, :, :], ["wq_d"], ["wqb"], eng="pool")
    dma(wkb[:, :, :].rearrange("p k c -> p (k c)").rearrange("p (o c) -> p o c", o=1), wk_d[:, :, :], ["wk_d"], ["wkb"], eng="pool")
    dma(wvb[:, :, :], wv_d[:, :, :], ["wv_d"], ["wvb"], eng="pool")
    dma(poolwb[:, :, :], poolw_d[:, :, :], ["poolw_d"], ["poolwb"], eng="pool")
    dma(ratio_bc[:, :], bass.AP(ratio, 0, [[0, 128], [1, 64]]), ["ratio_d"], ["ratio"])

    def rope_round(r):
        ops = []
        c_lo, c_hi = 2 * r, 2 * r + 1
        rn = "angf"

        def o(f):
            ops.append(f)
        o(lambda: dma(posi[0:64, :], bass.AP(pos, c_lo * 1024, [[0, 64], [1, 1024]]), ["pos_d"], ["posi_a"]))
        o(lambda: dma(posi[64:128, :], bass.AP(pos, c_hi * 1024, [[0, 64], [1, 1024]]), ["pos_d"], ["posi_b"]))
        o(lambda: cp("dve", angf[:, :], posi[:, :], ["posi_a", "posi_b"], [rn]))
        o(lambda: ts("dve", angf[:, :], angf[:, :], vecs[:, V_RA:V_RA + 1], None, ALU.mult, None, [rn, "vecs"], [rn]))
        o(lambda: ts("dve", ki[:, :], angf[:, :], 1.0 / TWO_PI, None, ALU.mult, None, [rn], ["ki"]))
        o(lambda: cp("dve", kf[:, :], ki[:, :], ["ki"], ["kf"]))
        o(lambda: stt(angf[:, :], kf[:, :], -CW1, angf[:, :], ALU.mult, ALU.add, ["kf", rn], [rn]))
        o(lambda: stt(angf[:, :], kf[:, :], -CW2, angf[:, :], ALU.mult, ALU.add, ["kf", rn], [rn]))
        o(lambda: cp("dve", kf[:, :], posi[:, :], ["posi_a", "posi_b", "kf"], ["kf"]))
        o(lambda: stt(angf[:, :], kf[:, :], vecs[:, V_RA2:V_RA2 + 1], angf[:, :], ALU.mult, ALU.add, ["kf", rn, "vecs"], [rn]))
        o(lambda: ts("dve", angf[:, :], angf[:, :], vecs[:, V_RB:V_RB + 1], None, ALU.add, None, [rn, "vecs"], [rn]))
        o(lambda: ts("dve", kf[:, :], angf[:, :], math.pi, TWO_PI, ALU.is_gt, ALU.mult, [rn], ["kf"]))
        o(lambda: tt("dve", angf[:, :], angf[:, :], kf[:, :], ALU.subtract, [rn, "kf"], [rn]))
        o(lambda: ts("dve", angf[:, :], angf[:, :], -PI_LO, PI_LO, ALU.max, ALU.min, [rn], [rn]))
        o(lambda: act(CS[0:64, c_lo * 1024:(c_lo + 1) * 1024], angf[0:64, :], AF.Sin, [rn], ["CS%d" % c_lo]))
        o(lambda: act(CS[0:64, c_hi * 1024:(c_hi + 1) * 1024], angf[64:128, :], AF.Sin, [rn], ["CS%d" % c_hi]))
        return ops

    for f_ in rope_round(0):
        f_()
    rope_later = rope_round(1)

    def rope_tick():
        if rope_later:
            rope_later.pop(0)()

    bank_rr = [0]

    def next_bank():
        b = bank_rr[0]
        bank_rr[0] = (b + 1) % 8
        return b

    proj_n = [0]

    def proj(bank, wt, col0, M, xb, xbn, wres, ncols=512, kcs=8):
        for kc in range(kcs):
            extra = ([xbn + "h"] + ([wres + "h"] if wres == "winb_q" else [])) if kc >= 4 else []
            mm(ps[bank][0:M, 0:ncols], wt[:, kc, col0:col0 + M], xb[:, kc, :], kc == 0, kc == kcs - 1,
               [wres, xbn] + extra, [PSN[bank]])
        proj_n[0] += 1
        if proj_n[0] > 11:
            rope_tick()

    def norm_chain(banks, sqs, nck, inv_n, gcol, dst, tcols, parity, dstname):
        bs = next_bank()
        for j in range(nck):
            mm(ps[bs][:, :], ones[:, :], sqs[j][:, :], j == 0, j == nck - 1, ["ones", "sq%d" % id(sqs[j])], [PSN[bs]])
        act(lnt[parity][:, :], ps[bs][:, :], AF.Ln, [PSN[bs]], ["lnt%d" % parity], scale=inv_n, bias=RMS_EPS)
        act(rstd[parity][:, :], lnt[parity][:, :], AF.Exp, ["lnt%d" % parity], ["rstd%d" % parity], scale=-0.5)
        for j in range(nck):
            stt(dst[:, j, tcols], ps[banks[j]][:, :], vecs[:, gcol + j:gcol + j + 1], rstd[parity][:, :],
                ALU.mult, ALU.mult, [PSN[banks[j]], "rstd%d" % parity, "vecs"], [dstname])

    def at(off, shape, dt, name):
        return nc.alloc_sbuf_tensor_at(name, list(shape), dt, offset=m_ab + off)

    khT = [at(O_RT, [128, S], BF16, "khT0"), at(0, [128, S], BF16, "khT1")]
    qhT = [at(O_RT + 8192, [128, T], BF16, "qhT0"), at(16384, [128, T], BF16, "qhT1")]
    Vh = [at(O_WP, [128, 32, 128], BF16, "Vh0"), at(8192, [128, 32, 128], BF16, "Vh1")]
    qtA = [at(O_RT + 12288, [128, 512], F32, "qtA0"), at(O_WQ, [128, 512], F32, "qtA1")]
    qtB = [at(O_RT + 14336, [128, 512], F32, "qtB0"), at(O_WQ + 2048, [128, 512], F32, "qtB1")]
    rc = [at(20480, [128, 512], F32, "rc0"), at(22528, [128, 512], F32, "rc1")]
    NPT = 5
    pT = [at(28672 + 1024 * i_, [128, 512], BF16, "pT%d" % i_) for i_ in range(NPT)]
    tmpE = at(44032, [128, 2064], F32, "tmpE")
    pg = [at(52288, [128, T], BF16, "pg0"), at(56384, [128, T], BF16, "pg1")]

    pj_rr = [0]
    pj_nb = [8]

    def pj_bank():
        if pj_nb[0] == 8:
            return next_bank()
        b = 6 + pj_rr[0] % 2
        pj_rr[0] += 1
        return b

    def q_piece(h, tg):
        hb = h % 2
        qhn = "qhT%d" % hb
        b = pj_bank()
        tcols = slice(tg * 512, (tg + 1) * 512)
        for kc in range(3):
            mm(ps[b][:, :], wqb[:, kc, h * 128:(h + 1) * 128], cqn[:, kc, tcols], kc == 0, kc == 2,
               ["wqb", "cqn%d" % tg], [PSN[b]])
        s_ = tg % 2
        cp("dve", qhT[hb][:, tcols], ps[b][:, :], [PSN[b]], [qhn])
        tt("dve", qtA[s_][0:32, :], ps[b][0:32, :], CS[0:32, tcols], ALU.mult, [PSN[b], "CS%d" % (tg // 2)], ["qtA%d" % s_])
        tt("dve", qtB[s_][0:32, :], ps[b][32:64, :], CS[32:64, tcols], ALU.mult, [PSN[b], "CS%d" % (tg // 2)], ["qtB%d" % s_])
        tt("dve", qhT[hb][0:32, tcols], qtA[s_][0:32, :], qtB[s_][0:32, :], ALU.add, ["qtA%d" % s_, "qtB%d" % s_, qhn], [qhn])

    def k_piece(h, kg):
        hb = h % 2
        khn = "khT%d" % hb
        b = pj_bank()
        kcols = slice(kg * 512, (kg + 1) * 512)
        for kc in range(2):
            mm(ps[b][:, :], wkb[:, kc, h * 128:(h + 1) * 128], ckvn[:, kc, kcols], kc == 0, kc == 1,
               ["wkb", "ckvn%d" % kg], [PSN[b]])
        if h == 0:
            act(khT[hb][64:128, kcols], ps[b][64:128, :], AF.Copy, [PSN[b]], [khn])
        else:
            cp("dve", khT[hb][64:128, kcols], ps[b][64:128, :], [PSN[b]], [khn])

    def krot_piece(h):
        hb = h % 2
        cp("dve", khT[hb][0:32, :], krot[0:32, :], ["krot%d" % i_ for i_ in range(8)], ["khT%d" % hb])

    def v_piece(h, kq):
        hb = h % 2
        vhn = "Vh%d" % hb
        b = pj_bank()
        for j in range(8):
            kt = kq * 8 + j
            for kc in range(2):
                mm(ps[b][:, j * 64:(j + 1) * 64], ckvn[:, kc, kt * 128:(kt + 1) * 128], wvb[:, kc, h * 64:(h + 1) * 64],
                   kc == 0, kc == 1, ["wvb", "ckvn%d" % (kt // 4)], [PSN[b]])
        if h == 0:
            act(Vh[hb][:, kq * 8:(kq + 1) * 8, 0:64], ps[b][:, :].rearrange("p (a b) -> p a b", b=64), AF.Copy, [PSN[b]], [vhn])
        else:
            cp("dve", Vh[hb][:, kq * 8:(kq + 1) * 8, 0:64], ps[b][:, :].rearrange("p (a b) -> p a b", b=64), [PSN[b]], [vhn])

    def pool_mm_piece(g, tg):
        b = pj_bank()
        mm(ps[b][:, :], poolwb[:, g, :], pg[g % 2][:, tg * 512:(tg + 1) * 512], True, True,
           ["poolwb", "pg%d" % (g % 2)], [PSN[b]])
        ts("dve", mixT[:, g, tg * 512:(tg + 1) * 512], ps[b][:, :], vecs[:, V_PSC + g:V_PSC + g + 1], None,
           ALU.mult, None, [PSN[b], "vecs"], ["mixT%d" % g])

    def head_pieces(h):
        pcs = []
        for tg in range(4):
            pcs.append(lambda tg=tg: q_piece(h, tg))
        pcs.append(lambda: krot_piece(h))
        for kg in range(8):
            pcs.append(lambda kg=kg: k_piece(h, kg))
        for kq in range(4):
            pcs.append(lambda kq=kq: v_piece(h, kq))
        return pcs


    for tg in range(8):
        own = tg < 4
        if tg == 4:
            while rope_later:
                rope_tick()
        xb = xTb[tg % NXB]
        xbn = "xTb%d" % (tg % NXB)
        tcols = slice(tg * 512, (tg + 1) * 512)
        if tg + 2 < 8:
            load_x(tg + 2)
        qb = []
        if own:
            for j in range(3):
                b = next_bank()
                proj(b, winb_q, j * 128, 128, xb, xbn, "winb_q")
                act(sq[j][:, :], ps[b][:, :], AF.Square, [PSN[b]], ["sq%d" % id(sq[j])])
                qb.append(b)
        kvb = []
        for j in range(2):
            b = next_bank()
            proj(b, winb_kv, j * 128, 128, xb, xbn, "winb_kv")
            act(sq[3 + j][:, :], ps[b][:, :], AF.Square, [PSN[b]], ["sq%d" % id(sq[3 + j])])
            kvb.append(b)
        if own:
            norm_chain(qb, sq[0:3], 3, 1.0 / 384, V_QG, cqn, tcols, 0, "cqn%d" % tg)
        br = next_bank()
        proj(br, winb_kv, 192, 128, xb, xbn, "winb_kv")
        norm_chain(kvb, sq[3:5], 2, 1.0 / 256, V_KVG, ckvn, tcols, 1, "ckvn%d" % tg)
        cidx = "CS%d" % (tg // 2)
        tt("dve", rtA[0:32, :], ps[br][64:96, :], CS[0:32, tcols], ALU.mult, [PSN[br], cidx], ["rtA"])
        tt("dve", rtB[0:32, :], ps[br][96:128, :], CS[32:64, tcols], ALU.mult, [PSN[br], cidx], ["rtB"])
        tt("dve", krot[0:32, tcols], rtA[0:32, :], rtB[0:32, :], ALU.add, ["rtA", "rtB"], ["krot%d" % tg])
        if own:
            for g in range(4):
                b = next_bank()
                proj(b, winb_p, g * 128, 128, xb, xbn, "winb_p")
                act(hpool[:, g, 8 + tg * 512:8 + (tg + 1) * 512], ps[b][:, :], AF.Copy, [PSN[b]], ["hpool%d" % g])
        if tg == 3:
            for g in range(4):
                b = next_bank()
                for kc in range(8):
                    mm(ps[b][:, 0:16], winb_p[:, kc, g * 128:(g + 1) * 128], xhb[:, kc, :], kc == 0, kc == 7,
                       ["winb_p", "xhb"], [PSN[b]])
                act(hpool[:, g, 0:8], ps[b][:, 0:8], AF.Copy, [PSN[b]], ["hpool%d" % g])
                act(hpool[:, g, 2056:2064], ps[b][:, 8:16], AF.Copy, [PSN[b]], ["hpool%d" % g])


            while rope_later:
                rope_tick()
            RT_N = ["posi_a", "posi_b", "angf", "kf", "ki"]
            memset("pool", khT[0][32:64, :], 0.0, ["khT0z", "khT0", "qhT0", "qtA0", "qtB0"] + RT_N)
            memset("dve", Vh[0][:, :, 64:128], 1.0, ["Vh0", "winb_p", "winb_q", "qtA1", "qtB1"])
            q_piece(0, 0)
            k_piece(0, 0)
        if tg == 4:
            q_piece(0, 1); k_piece(0, 1); k_piece(0, 2); v_piece(0, 0)
        if tg == 5:
            q_piece(0, 2); k_piece(0, 3); k_piece(0, 4); v_piece(0, 1)
        if tg == 6:
            q_piece(0, 3); k_piece(0, 5); k_piece(0, 6); v_piece(0, 2)
        if tg == 7:
            k_piece(0, 7); v_piece(0, 3); krot_piece(0)

    P.barrier()
    tap("hpool", hpool[:, :, :], [128, 4, 2064], F32)
    tap("cqn", cqn[:, :, :], [128, 3, T], BF16)
    tap("ckvn", ckvn[:, :, :], [128, 2, S], BF16)
    tap("krot", krot[0:32, :], [32, S], BF16)
    tap("CS", CS[0:64, :], [64, S], F32)
    P.barrier()
    def pool_group(g):
        w = (2, 4, 8, 16)[g]
        U = hpool[:, g, :]
        hn = "hpool%d" % g
        bufs = [(tmpE[:, :], "tmpE")]
        if g >= 1:
            bufs.append((hpool[:, g - 1, :], "hpool%d" % (g - 1)))
        cur, curn = bufs[0]
        tt("pool", cur[:, 1:2064], U[:, 0:2063], U[:, 1:2064], ALU.add, [hn, "att_done%d" % g], [curn])
        lo, hi, sh = 1, 2064, 1
        nb = 1
        while sh * 2 < w:
            nxt, nxtn = bufs[nb % 2] if len(bufs) > 1 else bufs[0]
            lo2, hi2 = lo + sh, hi - sh
            tt("pool", nxt[:, lo2:hi2], cur[:, lo2 - sh:hi2 - sh], cur[:, lo2 + sh:hi2 + sh], ALU.add, [curn], [nxtn])
            cur, curn = nxt, nxtn
            lo, hi = lo2, hi2
            sh *= 2
            nb += 1
        assert lo <= 8 and hi >= 2056
        win = cur
        ts("pool", win[:, 8:2056], win[:, 8:2056], 1.0 / w, 0.0, ALU.mult, ALU.add, [curn], [curn])
        tt("pool", win[:, 8:16], win[:, 8:16], ratio_bc[:, g * 16:g * 16 + 8], ALU.mult, [curn, "ratio"], [curn])
        tt("pool", win[:, 2048:2056], win[:, 2048:2056], ratio_bc[:, g * 16 + 8:g * 16 + 16], ALU.mult, [curn, "ratio"], [curn])
        tt("pool", pg[g % 2][:, :], win[:, 8:2056], U[:, 8:2056], ALU.subtract, [curn, hn], ["pg%d" % (g % 2)])

    NSC = 4
    LAG = 3

    def attention(h, tg, pieces):
        hb = h % 2
        khn, qhn, vhn = "khT%d" % hb, "qhT%d" % hb, "Vh%d" % hb
        tcols = slice(tg * 512, (tg + 1) * 512)
        ob = 4 + (tg % 2)

        def pv(kt):
            mm(ps[ob][:, :], Vh[hb][:, kt, :], pT[kt % NPT][:, :], kt == 0, kt == 31,
               [vhn, "pT%d" % (kt % NPT)], [PSN[ob]])

        for kt in range(32):
            sb = kt % NSC
            mm(ps[sb][:, :], khT[hb][:, kt * 128:(kt + 1) * 128], qhT[hb][:, tcols], True, True, [khn, khn + "z", qhn], [PSN[sb]])
            act(pT[kt % NPT][:, :], ps[sb][:, :], AF.Exp, [PSN[sb]], ["pT%d" % (kt % NPT)], scale=SM_SCALE)
            if kt >= LAG:
                pv(kt - LAG)
            if kt % 7 == 3 and pieces:
                pieces.pop(0)()
        for kt in range(32 - LAG, 32):
            pv(kt)
        s_ = tg % 2
        P.op("dve", lambda e: e.reciprocal(out=rc[s_][0:64, :], in_=ps[ob][64:128, :]), [PSN[ob]], ["rc%d" % s_])
        po = (h % 2) * 64
        tt("dve", mixT[po:po + 64, 4 + h // 2, tcols], ps[ob][0:64, :], rc[s_][0:64, :], ALU.mult,
           [PSN[ob], "rc%d" % s_], ["mixT%d" % (4 + h // 2)] + (["att_done%d" % h] if tg == 0 else []))

    hp_off = m_whole
    wob_pre = nc.alloc_sbuf_tensor_at("wob_pre", [128, 8, D], BF16, offset=hp_off)
    lnbc1_pre = nc.alloc_sbuf_tensor_at("lnbc1_pre", [128, 2, D], F32, offset=hp_off + 16384)
    lnbc2_pre = nc.alloc_sbuf_tensor_at("lnbc2_pre", [128, 2, D], F32, offset=hp_off + 24576)
    HPR = ["hpool%d" % g_ for g_ in range(4)] + ["tmpE"]

    def prefetch_c():
        dma(wob_pre[:, :, :].rearrange("p (n a) c -> p n (a c)", n=4), wo_d[:, :, :], ["wo_d"] + HPR, HPR + ["wob"], eng="pool")
        for i_ in range(2):
            dma(lnbc1_pre[:, i_, :], bass.AP(lnp, i_ * D, [[0, 128], [1, D]]), ["lnp"] + HPR, HPR + ["lnbc1_%d" % i_])
            dma(lnbc2_pre[:, i_, :], bass.AP(lnp, (2 + i_) * D, [[0, 128], [1, D]]), ["lnp"] + HPR, HPR + ["lnbc2_%d" % i_])
        ts("pool", lnbc1_pre[:, 0, :], lnbc1_pre[:, 0, :], ALPHA, 0.0, ALU.mult, ALU.add, ["lnbc1_0"], ["lnbc1_0"])
        ts("pool", lnbc1_pre[:, 1, :], lnbc1_pre[:, 1, :], ALPHA, 0.0, ALU.mult, ALU.add, ["lnbc1_1"], ["lnbc1_1"])

    pj_nb[0] = 2
    init1 = [lambda: memset("pool", khT[1][32:64, :], 0.0, ["khT1z"]),
             lambda: memset("pool", Vh[1][:, :, 64:128], 1.0, ["Vh1"])]
    def att_sc(g, h, tg, kt):
        hb = h % 2
        khn, qhn = "khT%d" % hb, "qhT%d" % hb
        tcols = slice(tg * 512, (tg + 1) * 512)
        sb = g % NSC
        mm(ps[sb][:, :], khT[hb][:, kt * 128:(kt + 1) * 128], qhT[hb][:, tcols], True, True, [khn, khn + "z", qhn], [PSN[sb]])
        act(pT[g % NPT][:, :], ps[sb][:, :], AF.Exp, [PSN[sb]], ["pT%d" % (g % NPT)], scale=SM_SCALE)

    def att_pv(g, h, tg, kt):
        hb = h % 2
        vhn = "Vh%d" % hb
        tcols = slice(tg * 512, (tg + 1) * 512)
        ob = 4 + (tg % 2)
        mm(ps[ob][:, :], Vh[hb][:, kt, :], pT[g % NPT][:, :], kt == 0, kt == 31,
           [vhn, "pT%d" % (g % NPT)], [PSN[ob]])
        if kt == 31:
            s_ = tg % 2
            P.op("dve", lambda e: e.reciprocal(out=rc[s_][0:64, :], in_=ps[ob][64:128, :]), [PSN[ob]], ["rc%d" % s_])
            po = (h % 2) * 64
            tt("dve", mixT[po:po + 64, 4 + h // 2, tcols], ps[ob][0:64, :], rc[s_][0:64, :], ALU.mult,
               [PSN[ob], "rc%d" % s_], ["mixT%d" % (4 + h // 2)] + (["att_done%d" % h] if tg == 0 else []))
            if tg == 0 and h < 4:
                pool_group(h)

    pend = []
    g = 0
    pieces = []
    for h in range(8):
        if h == 6:
            prefetch_c()
        pieces = head_pieces(h + 1) if h + 1 < 8 else []
        if h == 0:
            pieces = init1 + pieces
        if 1 <= h <= 4:
            pieces = pieces + [lambda tg=tg, g_=h - 1: pool_mm_piece(g_, tg) for tg in range(4)]
        for tg in range(4):
            for kt in range(32):
                att_sc(g, h, tg, kt)
                pend.append((g, h, tg, kt))
                g += 1
                if len(pend) > LAG:
                    att_pv(*pend.pop(0))
                if kt % 7 == 3 and pieces:
                    pieces.pop(0)()
        while pieces:
            pieces.pop(0)()
    while pend:
        att_pv(*pend.pop(0))

    P.barrier()
    tap("mixT", mixT[:, :, :], [128, 8, T], BF16)
    P.barrier()
    A.release(m_whole)
    wob = wob_pre
    lnbc1 = lnbc1_pre
    lnbc2 = lnbc2_pre
    A.ptr = m_whole + 33024
    acc = A.tile([128, 16, D], F32, "acc")
    x1T = A.tile([128, 8, T], BF16, "x1T")
    NWG = 6
    NWD = 8
    wgub = [A.tile([128, 2, 8, 128], BF16, "wgub") for _ in range(NWG)]
    wdb = [A.tile([128, D], BF16, "wdb") for _ in range(6)]
    xt_off = A.mark()
    xt = [A.tile([128, D], F32, "xt") for _ in range(2)]
    wdb.append(nc.alloc_sbuf_tensor_at("wdb6", [128, D], BF16, offset=xt_off))
    wdb.append(nc.alloc_sbuf_tensor_at("wdb7", [128, D], BF16, offset=xt_off + 4096))
    WDN = ["wdb%d" % i_ for i_ in range(6)] + ["xt0", "xt1"]
    sg = [A.tile([128, 512], BF16, "sg") for _ in range(2)]
    NST = 4
    st = [A.tile([128, 12], F32, "st") for _ in range(NST)]
    mv = [A.tile([128, 2], F32, "mv") for _ in range(NST)]
    sm = [A.tile([128, 4], F32, "sm") for _ in range(NST)]
    def hact_ap(hb, j, cols):
        return mixT[:, hb * 4 + j, cols]

    def MR(c, tg):
        return "M%d_%d" % (c, tg)


    def load_gu(fc):
        s_ = fc % NWG
        dma(wgub[s_][:, :, :, :], wgu_d[fc], ["wgu_d"], ["wgb%d" % s_], eng="pool")

    def load_d(fc):
        s_ = fc % NWD
        dma(wdb[s_][:, :], wd_d[fc], ["wd_d"], [WDN[s_]], eng="pool")

    for fc in range(NWG):
        load_gu(fc)
    for fc in range(6):
        load_d(fc)

    def ln_stats_a(src, srcn, i):
        st_i, mv_i, sm_i = st[i], mv[i], sm[i]
        P.op("dve", lambda e: e.bn_stats(out=st_i[:, 0:6], in_=src[:, 0:512]), [srcn], ["st%d" % i])
        P.op("dve", lambda e: e.bn_stats(out=st_i[:, 6:12], in_=src[:, 512:1024]), [srcn], ["st%d" % i])
        P.op("dve", lambda e: e.bn_aggr(out=mv_i[:, 0:2], in_=st_i[:, 0:12]), ["st%d" % i], ["mv%d" % i])
        ts("pool", sm_i[:, 2:3], mv_i[:, 1:2], LN_EPS, 1.0, ALU.add, ALU.mult, ["mv%d" % i], ["sm%d" % i])
        tt("pool", sm_i[:, 0:1], sm_i[:, 2:3], neghalf[:, 0:1], ALU.pow, ["sm%d" % i, "neghalf"], ["sm%d" % i])

    def ln_stats_b(i):
        mv_i, sm_i = mv[i], sm[i]
        ts("dve", sm_i[:, 1:2], mv_i[:, 0:1], sm_i[:, 0:1], -1.0, ALU.mult, ALU.mult, ["mv%d" % i, "sm%d" % i], ["sm%d" % i])

    def normalize(out, in_, i, reads, writes):
        sm_i = sm[i]
        P.op("act", lambda e: e.activation(out=out, in_=in_, func=AF.Identity, scale=sm_i[:, 0:1], bias=sm_i[:, 1:2]),
             reads, writes)

    groups = [[0, 1, 2, 3], [4, 5, 6, 7], [8, 9, 10, 11], [12, 13], [14, 15, 16, 17], [18, 19, 20, 21]]
    gu_rr = [0]
    sg_rr = [0]
    dn_rr = [0]

    def gate_up(fg, j, tg):
        fc = groups[fg][j]
        s = fc % NWG
        hb = fg % 2
        tcols = slice(tg * 512, (tg + 1) * 512)
        pair = gu_rr[0] % 2
        gu_rr[0] += 1
        base_b = 4 if fg == 0 else 0
        bg, bu = base_b + 2 * pair, base_b + 2 * pair + 1
        for kc in range(8):
            mm(ps[bg][:, :], wgub[s][:, 0, kc, :], x1T[:, kc, tcols], kc == 0, kc == 7,
               ["wgb%d" % s, "x1T%d_%d" % (tg, kc)], [PSN[bg]])
        for kc in range(8):
            mm(ps[bu][:, :], wgub[s][:, 1, kc, :], x1T[:, kc, tcols], kc == 0, kc == 7,
               ["wgb%d" % s, "x1T%d_%d" % (tg, kc)], [PSN[bu]])
        si = sg_rr[0] % 2
        sg_rr[0] += 1
        act(sg[si][:, :], ps[bg][:, :], AF.Silu, [PSN[bg]], ["sg%d" % si])
        tt("dve", hact_ap(hb, j, tcols), ps[bu][:, :], sg[si][:, :], ALU.mult, [PSN[bu], "sg%d" % si], [MR(hb * 4 + j, tg)])

    def down(fg, t_, half):
        hb = fg % 2
        fcs = groups[fg]
        b = 4 + dn_rr[0] % 4
        dn_rr[0] += 1
        tok = slice(t_ * 128, (t_ + 1) * 128)
        hc = slice(half * 512, (half + 1) * 512)
        for j, fc in enumerate(fcs):
            mm(ps[b][:, :], hact_ap(hb, j, tok), wdb[fc % NWD][:, hc], j == 0, j == len(fcs) - 1,
               [MR(hb * 4 + j, t_ // 4), WDN[fc % NWD]], [PSN[b]])
        tt("dve", acc[:, t_, hc], ps[b][:, :], acc[:, t_, hc], ALU.add, [PSN[b], "acc%d_%d" % (t_, half)], ["acc%d_%d" % (t_, half)])

    def gu_items(fg):
        return [(fg, j, tg) for tg in range(4) for j in range(len(groups[fg]))]

    def dn_items(fg):
        return [(fg, t_, half) for t_ in range(16) for half in range(2)]

    def c1_load(t_):
        i = t_ % 2
        dma(xt[i][:, :], xown[t_ * 128:(t_ + 1) * 128, :], ["xown"], ["xt%d" % i])

    def c1_mix(t_):
        i = t_ % 2
        tok = slice(t_ * 128, (t_ + 1) * 128)
        for half in range(2):
            b = half
            hc = slice(half * 512, (half + 1) * 512)
            for kc in range(8):
                mm(ps[b][:, :], mixT[:, kc, tok], wob[:, kc, hc], kc == 0, kc == 7,
                   [MR(kc, t_ // 4), "wob"], [PSN[b]])
            stt(acc[:, t_, hc], xt[i][:, hc], ALPHA, ps[b][:, :],
                ALU.mult, ALU.add, ["xt%d" % i, PSN[b]], ["acc%d_%d" % (t_, half)])
        j_ = t_ % NST
        src = acc[:, t_, :]
        st_i, mv_i, sm_i = st[j_], mv[j_], sm[j_]
        P.op("dve", lambda e: e.bn_stats(out=st_i[:, 0:6], in_=src[:, 0:512]), ["acc%d_0" % t_], ["st%d" % j_])
        P.op("dve", lambda e: e.bn_stats(out=st_i[:, 6:12], in_=src[:, 512:1024]), ["acc%d_1" % t_], ["st%d" % j_])
        P.op("dve", lambda e: e.bn_aggr(out=mv_i[:, 0:2], in_=st_i[:, 0:12]), ["st%d" % j_], ["mv%d" % j_])
        ts("pool", sm_i[:, 2:3], mv_i[:, 1:2], LN_EPS, 1.0, ALU.add, ALU.mult, ["mv%d" % j_], ["sm%d" % j_])
        tt("pool", sm_i[:, 0:1], sm_i[:, 2:3], neghalf[:, 0:1], ALU.pow, ["sm%d" % j_, "neghalf"], ["sm%d" % j_])

    def c1_norm(t_):
        an2 = ["acc%d_0" % t_, "acc%d_1" % t_]
        ln_stats_b(t_ % NST)
        normalize(acc[:, t_, :], acc[:, t_, :], t_ % NST, an2 + ["sm%d" % (t_ % NST)], an2)

    def c1_init(t_):
        an2 = ["acc%d_0" % t_, "acc%d_1" % t_]
        tt("pool", acc[:, t_, :], acc[:, t_, :], lnbc1[:, 0, :], ALU.mult, an2 + ["lnbc1_0"], an2)
        tt("pool", acc[:, t_, :], acc[:, t_, :], lnbc1[:, 1, :], ALU.add, an2 + ["lnbc1_1"], an2)

    def c1_tr(t_):
        i = t_ % 2
        tok = slice(t_ * 128, (t_ + 1) * 128)
        for q4 in range(2):
            b = 2 + q4
            for j in range(4):
                kc = q4 * 4 + j
                P.op("pe", lambda e, o_=ps[b][:, j * 128:(j + 1) * 128], i_=acc[:, t_, kc * 128:(kc + 1) * 128]:
                     e.transpose(out=o_, in_=i_, identity=ident[:, :]), ["acc%d_%d" % (t_, kc // 4), "ident"], [PSN[b]])
            for j in range(4):
                kc = q4 * 4 + j
                if q4 == 0:
                    ts("dve", x1T[:, kc, tok], ps[b][:, j * 128:(j + 1) * 128], vecs[:, V_G1 + kc:V_G1 + kc + 1],
                       vecs[:, V_B1 + kc:V_B1 + kc + 1], ALU.mult, ALU.add, [PSN[b], "vecs"], ["x1T%d_%d" % (t_ // 4, kc)])
                else:
                    P.op("act", lambda e, o_=x1T[:, kc, tok], i_=ps[b][:, j * 128:(j + 1) * 128],
                         s_=vecs[:, V_G1 + kc:V_G1 + kc + 1], b_=vecs[:, V_B1 + kc:V_B1 + kc + 1]:
                         e.activation(out=o_, in_=i_, func=AF.Identity, scale=s_, bias=b_),
                         [PSN[b], "vecs"], ["x1T%d_%d" % (t_ // 4, kc)])

    ready_gu = []
    gu0 = gu_items(0)
    c1_load(0)
    c1_load(1)
    for k in range(16 + 2):
        if 1 <= k <= 16:
            c1_norm(k - 1)
        if k < 16:
            c1_mix(k)
            if k + 2 < 16:
                c1_load(k + 2)
        if ready_gu:
            gate_up(*ready_gu.pop(0))
        if 2 <= k <= 17:
            t_ = k - 2
            c1_tr(t_)
            if t_ % 4 == 3:
                tg = t_ // 4
                ready_gu += [it for it in gu0 if it[2] == tg]
        if 3 <= k:
            c1_init(k - 3)
    c1_init(15)
    while ready_gu:
        gate_up(*ready_gu.pop(0))
    for fc in groups[0]:
        if fc + NWG < NFC:
            load_gu(fc + NWG)
    load_d(6)
    load_d(7)

    tap("x1T", x1T[:, :, :], [128, 8, T], BF16, reads=["x1T%d_%d" % (g_, k_) for g_ in range(4) for k_ in range(8)])

    def ln2_a(t_):
        i = t_ % NST
        src = acc[:, t_, :]
        st_i, mv_i, sm_i = st[i], mv[i], sm[i]
        P.op("dve", lambda e: e.bn_stats(out=st_i[:, 0:6], in_=src[:, 0:512]), ["acc%d_0" % t_], ["st%d" % i])
        P.op("dve", lambda e: e.bn_stats(out=st_i[:, 6:12], in_=src[:, 512:1024]), ["acc%d_1" % t_], ["st%d" % i])
        P.op("dve", lambda e: e.bn_aggr(out=mv_i[:, 0:2], in_=st_i[:, 0:12]), ["st%d" % i], ["mv%d" % i])
        ts("pool", sm_i[:, 2:3], mv_i[:, 1:2], LN_EPS, 1.0, ALU.add, ALU.mult, ["mv%d" % i], ["sm%d" % i])
        tt("pool", sm_i[:, 0:1], sm_i[:, 2:3], neghalf[:, 0:1], ALU.pow, ["sm%d" % i, "neghalf"], ["sm%d" % i])

    def ln2_b(t_):
        an2 = ["acc%d_0" % t_, "acc%d_1" % t_]
        ln_stats_b(t_ % NST)
        normalize(acc[:, t_, :], acc[:, t_, :], t_ % NST, an2 + ["sm%d" % (t_ % NST)], an2)

    def ln2_c(t_):
        an2 = ["acc%d_0" % t_, "acc%d_1" % t_]
        tok = slice(t_ * 128, (t_ + 1) * 128)
        tt("dve", acc[:, t_, :], acc[:, t_, :], lnbc2[:, 0, :], ALU.mult, an2 + ["lnbc2_0"], an2)
        tt("dve" if t_ >= 14 else "pool", acc[:, t_, :], acc[:, t_, :], lnbc2[:, 1, :], ALU.add, an2 + ["lnbc2_1"], an2)
        dma(y[tok, :], acc[:, t_, :], an2, ["y%d" % t_])

    def down_final(t_, half):
        b = 4 + dn_rr[0] % 4
        dn_rr[0] += 1
        tok = slice(t_ * 128, (t_ + 1) * 128)
        hc = slice(half * 512, (half + 1) * 512)
        items = [(4, j, fc) for j, fc in enumerate(groups[4])] + [(5, j, fc) for j, fc in enumerate(groups[5])]
        for n_, (fg_, j, fc) in enumerate(items):
            hb = fg_ % 2
            mm(ps[b][:, :], hact_ap(hb, j, tok), wdb[fc % NWD][:, hc], n_ == 0, n_ == len(items) - 1,
               [MR(hb * 4 + j, t_ // 4), WDN[fc % NWD]], [PSN[b]])
        tt("dve", acc[:, t_, hc], ps[b][:, :], acc[:, t_, hc], ALU.add, [PSN[b], "acc%d_%d" % (t_, half)], ["acc%d_%d" % (t_, half)])

    for fg in range(4):
        gus = gu_items(fg + 1)
        dns = dn_items(fg)
        gi = 0
        for _ in range(3):
            if gi < len(gus):
                gate_up(*gus[gi])
                gi += 1
        for di, d_ in enumerate(dns):
            down(*d_)
            if gi < len(gus):
                gate_up(*gus[gi])
                gi += 1
        while gi < len(gus):
            gate_up(*gus[gi])
            gi += 1
        for fc in groups[fg + 1]:
            if fc + NWG < NFC:
                load_gu(fc + NWG)
        for fc in groups[fg]:
            if fc + NWD < NFC:
                load_d(fc + NWD)
    gu5 = gu_items(5)
    gi5 = 0
    for _ in range(4):
        gate_up(*gu5[gi5])
        gi5 += 1
    for t_ in range(16):
        for half in range(2):
            down_final(t_, half)
        ln2_a(t_)
        if t_ >= 1:
            ln2_b(t_ - 1)
        if t_ >= 2:
            ln2_c(t_ - 2)
        if gi5 < len(gu5):
            gate_up(*gu5[gi5])
            gi5 += 1
    ln2_b(15)
    ln2_c(14)
    ln2_c(15)

    P.op("sp", lambda e: e.nop(), ["y%d" % t_ for t_ in range(16)] + ["dbg_" + k for k in taps], ())
    P.emit()
    return nc


def _host_layouts(inp):
    x = np.asarray(inp["x"], dtype=np.float32)
    positions = np.asarray(inp["positions"]).astype(np.int32)
    w_in = np.asarray(inp["w_in"], dtype=np.float32)[0]
    rope = w_in[:, 1152:1184]
    w_in_ext = np.concatenate([w_in, rope[:, 16:32], rope[:, 0:16]], axis=1)
    w_in_l = w_in_ext.reshape(8, 128, WIN_COLS).transpose(1, 0, 2)
    winq_l = np.ascontiguousarray(w_in_l[:, :, 512:896]).reshape(128, 2, 1536)
    winkv_l = np.ascontiguousarray(w_in_l[:, :, 896:WIN_COLS]).reshape(128, 2, 1280)
    winp_l = np.ascontiguousarray(w_in_l[:, :, 0:512]).reshape(128, 2, 2048)
    pool_w_l = np.ascontiguousarray(np.asarray(inp["pool_w"], dtype=np.float32)[0].transpose(1, 0, 2))
    wq = np.asarray(inp["w_q_up"], dtype=np.float32)[0].reshape(384, 8, 96)
    wq_h = np.concatenate([wq[:, :, 64:96], wq[:, :, 80:96], wq[:, :, 64:80], wq[:, :, 0:64]], axis=2)
    wq_l = np.ascontiguousarray(wq_h.reshape(3, 128, 1024).transpose(1, 0, 2)).reshape(128, 2, 1536)
    wk4 = np.asarray(inp["w_k_up"], dtype=np.float32)[0].reshape(2, 128, 8, 64).transpose(1, 0, 2, 3)
    wk_pad = np.zeros((128, 2, 8, 128), np.float32)
    wk_pad[:, :, :, 64:128] = wk4
    wk_l = np.ascontiguousarray(wk_pad).reshape(128, 1, 2048)
    wv_l = np.ascontiguousarray(np.asarray(inp["w_v_up"], dtype=np.float32)[0].reshape(2, 128, 512).transpose(1, 0, 2))
    wo_l = np.ascontiguousarray(np.asarray(inp["w_o"], dtype=np.float32)[0].reshape(8, 128, 1024).transpose(1, 0, 2)).reshape(128, 4, 2048)
    wg_l = np.asarray(inp["w_gate"], dtype=np.float32)[0].reshape(8, 128, NFC, 128).transpose(2, 1, 0, 3)
    wu_l = np.asarray(inp["w_up"], dtype=np.float32)[0].reshape(8, 128, NFC, 128).transpose(2, 1, 0, 3)
    wgu_l = np.ascontiguousarray(np.stack([wg_l, wu_l], axis=2))
    wd_l = np.ascontiguousarray(np.asarray(inp["w_down"], dtype=np.float32)[0].reshape(NFC, 128, 1024))
    lnp = np.ascontiguousarray(np.stack([np.asarray(inp[k], dtype=np.float32)[0] for k in ("ln1_g", "ln1_b", "ln2_g", "ln2_b")]))
    vecs = np.zeros((128, NVEC), np.float32)
    vecs[:, V_PSC:V_PSC + 4] = np.asarray(inp["pool_scale"], dtype=np.float32)[0].reshape(4, 128).T
    vecs[:, V_QG:V_QG + 3] = np.asarray(inp["q_norm_g"], dtype=np.float32)[0].reshape(3, 128).T
    vecs[:, V_KVG:V_KVG + 2] = np.asarray(inp["kv_norm_g"], dtype=np.float32)[0].reshape(2, 128).T
    vecs[:, V_G1:V_G1 + 8] = np.asarray(inp["ln1_g"], dtype=np.float32)[0].reshape(8, 128).T
    vecs[:, V_B1:V_B1 + 8] = np.asarray(inp["ln1_b"], dtype=np.float32)[0].reshape(8, 128).T
    inv64 = 1.0 / (10000.0 ** (np.arange(0, 32, 2, dtype=np.float64) / 32.0))
    a32 = inv64.astype(np.float32)
    a_hi = (a32.view(np.uint32) & np.uint32(0xFFFFF000)).view(np.float32)
    a_lo = (inv64 - a_hi.astype(np.float64)).astype(np.float32)
    a = np.zeros(128, np.float32)
    a2 = np.zeros(128, np.float32)
    b = np.zeros(128, np.float32)
    for r0 in (0, 16, 32, 48):
        a[r0:r0 + 16] = a_hi
        a2[r0:r0 + 16] = a_lo
    b[0:32] = np.float32(math.pi / 2)
    b[32:48] = np.float32(math.pi)
    a[64:128] = a[0:64]
    a2[64:128] = a2[0:64]
    b[64:128] = b[0:64]
    vecs[:, V_RA2] = a2
    vecs[:, V_RA] = a
    vecs[:, V_RB] = b
    shared = dict(vecs=vecs, winq_l=winq_l, winkv_l=winkv_l, winp_l=winp_l, pool_w_l=pool_w_l, wq_l=wq_l, wk_l=wk_l, wv_l=wv_l, wo_l=wo_l,
                  wgu_l=wgu_l, wd_l=wd_l, lnp=lnp)
    in_maps = []
    for c in range(8):
        b_, half = c // 2, c % 2
        o0, o1 = half * T, (half + 1) * T
        r0, r1 = (1 - half) * T, (2 - half) * T
        xo = x[b_, o0:o1]
        xr = x[b_, r0:r1]
        xT = np.concatenate([xo, xr], axis=0).T
        xT_l = np.ascontiguousarray(xT.reshape(8, 128, 8, 512).transpose(2, 1, 0, 3)).reshape(8, 128, 2, 2048)
        xh = np.zeros((16, D), np.float32)
        if o0 >= 8:
            xh[0:8] = x[b_, o0 - 8:o0]
        if o1 + 8 <= S:
            xh[8:16] = x[b_, o1:o1 + 8]
        pos = np.concatenate([positions[b_, o0:o1], positions[b_, r0:r1]])[None, :].astype(np.int32)
        ratio = np.ones((4, 16), np.float32)
        for g, w in enumerate((2, 4, 8, 16)):
            for k in range(16):
                gi = o0 + k if k < 8 else o1 - 16 + k
                lo = max(gi - w // 2, 0)
                hi = min(gi + w - w // 2, S)
                ratio[g, k] = np.float32(w) / np.float32(hi - lo)
        m = dict(shared)
        m.update(xT_l=xT_l, xown=np.ascontiguousarray(xo),
                 xh_l=np.ascontiguousarray(xh.T.reshape(8, 128, 16).transpose(1, 0, 2)).reshape(128, 128), pos=np.ascontiguousarray(pos),
                 ratio=np.ascontiguousarray(ratio.reshape(1, 64)))
        in_maps.append(m)
    return in_maps


_NC_CACHE = {}


def kernel(**inputs):
    in_maps = _host_layouts(inputs)
    if "nc" not in _NC_CACHE:
        _NC_CACHE["nc"] = build_nc()
    nc = _NC_CACHE["nc"]
    res = run_bass_kernel_spmd(nc, in_maps, core_ids=list(range(8)))
    out = np.zeros((4, S, D), np.float32)
    for c in range(8):
        b_, half = c // 2, c % 2
        out[b_, half * T:(half + 1) * T] = np.asarray(res.results[c]["y"], dtype=np.float32)
    return out
```
